# Optimizing a Trainium2 kernel written in Bass

```python
import math
import jax, jax.numpy as jnp
from jax import lax
import numpy as np

D_MODEL = 2048
BATCH = 4
SEQ = 4096
DEPTH = 1
DEC_BATCH = 2
DEC_SEQ = 16384
PAST_LEN = 128

N_Q_HEADS = 8
N_KV_HEADS = 2
HEAD_DIM = 128
WINDOW = 128
BLOCK = 128
ATTN_WIDTH = N_Q_HEADS * HEAD_DIM
KV_WIDTH = N_KV_HEADS * HEAD_DIM
CONV_WIDTH = 1024
CONV_K = 3
N_MEM = 256
MEM_HEADS = 4
MEM_HEAD_DIM = 256
MEM_WIDTH = MEM_HEADS * MEM_HEAD_DIM
N_BRANCHES = 3
PEER_HEADS = 8
N_KEYS = 128
N_EXPERTS = N_KEYS * N_KEYS
PEER_HALF = 128
PEER_QDIM = 2 * PEER_HALF
PEER_TOPK = 16
PEER_CHUNK = 128
LN_EPS = 1e-5
SPLIT_SIZES = (ATTN_WIDTH, KV_WIDTH, KV_WIDTH, CONV_WIDTH, CONV_WIDTH, CONV_WIDTH, MEM_WIDTH, N_BRANCHES * D_MODEL)
IN_WIDTH = sum(SPLIT_SIZES)

kernel_name = 'hybrid_swa_conv_mem_peer_encoder'


def layer_norm(x, g, b):
    xf = x.astype(jnp.float32)
    mu = xf.mean(-1, keepdims=True)
    var = jnp.square(xf - mu).mean(-1, keepdims=True)
    return ((xf - mu) * lax.rsqrt(var + LN_EPS)).astype(x.dtype) * g + b


def alibi_slopes(n_heads):
    return jnp.exp2(-8.0 * jnp.arange(1, n_heads + 1, dtype=jnp.float32) / n_heads)


def windowed_gqa(q, k, v, sink):
    B, S = q.shape[0], q.shape[1]
    nb = S // BLOCK
    G = N_Q_HEADS // N_KV_HEADS
    qb = q.reshape(B, nb, BLOCK, N_KV_HEADS, G, HEAD_DIM)

    def band(t):
        tp = jnp.pad(t, ((0, 0), (BLOCK, BLOCK), (0, 0), (0, 0)))
        tp = tp.reshape(B, nb + 2, BLOCK, N_KV_HEADS, HEAD_DIM)
        return jnp.concatenate([tp[:, :-2], tp[:, 1:-1], tp[:, 2:]], axis=2)

    kb, vb = band(k), band(v)
    s = jnp.einsum('bnqkgd,bnskd->bnkgqs', qb, kb).astype(jnp.float32) / math.sqrt(HEAD_DIM)
    qpos = jnp.arange(nb)[:, None] * BLOCK + jnp.arange(BLOCK)[None, :]
    kpos = jnp.arange(nb)[:, None] * BLOCK - BLOCK + jnp.arange(3 * BLOCK)[None, :]
    dist = jnp.abs(qpos[:, :, None] - kpos[:, None, :])
    valid = (dist <= WINDOW) & (kpos[:, None, :] >= 0) & (kpos[:, None, :] < S)
    slopes = alibi_slopes(N_Q_HEADS).reshape(N_KV_HEADS, G)
    bias = -slopes[None, :, :, None, None] * dist[:, None, None].astype(jnp.float32)
    s = jnp.where(valid[:, None, None], s + bias, -jnp.inf)
    sk = sink.astype(jnp.float32).reshape(N_KV_HEADS, G)[:, :, None, None]
    m = jnp.maximum(s.max(-1, keepdims=True), sk)
    p = jnp.exp(s - m)
    denom = p.sum(-1, keepdims=True) + jnp.exp(sk - m)
    p = (p / denom).astype(v.dtype)
    o = jnp.einsum('bnkgqs,bnskd->bnqkgd', p, vb)
    return o.reshape(B, S, ATTN_WIDTH)


def short_gated_conv(h, b_gate, c_gate, conv_w):
    S = h.shape[1]
    z = c_gate * h
    zp = jnp.pad(z, ((0, 0), (1, 1), (0, 0)))
    conv = zp[:, :S] * conv_w[0] + zp[:, 1:S + 1] * conv_w[1] + zp[:, 2:] * conv_w[2]
    return b_gate * conv


def memory_cross_attn(q, mk, mv):
    B, S = q.shape[0], q.shape[1]
    s = jnp.einsum('bqhd,bmhd->bhqm', q, mk).astype(jnp.float32) / math.sqrt(MEM_HEAD_DIM)
    p = jax.nn.softmax(s, axis=-1).astype(mv.dtype)
    o = jnp.einsum('bhqm,bmhd->bqhd', p, mv)
    return o.reshape(B, S, MEM_WIDTH)


def peer_ffn(x, w_q, sub_keys, u_tab, v_tab):
    B, S, D = x.shape
    xt = x.reshape(-1, PEER_CHUNK, D)

    def chunk(xc):
        C = xc.shape[0]
        q = (xc @ w_q).reshape(C, PEER_HEADS, 2, PEER_HALF)
        sc = jnp.einsum('chpd,hpkd->chpk', q, sub_keys).astype(jnp.float32)
        top_s, top_i = lax.top_k(sc, PEER_TOPK)
        cand_s = (top_s[:, :, 0, :, None] + top_s[:, :, 1, None, :]).reshape(C, PEER_HEADS, -1)
        cand_i = (top_i[:, :, 0, :, None] * N_KEYS + top_i[:, :, 1, None, :]).reshape(C, PEER_HEADS, -1)
        best_s, pos = lax.top_k(cand_s, PEER_TOPK)
        idx = jnp.take_along_axis(cand_i, pos, axis=-1)
        g = jax.nn.softmax(best_s, axis=-1)
        u = u_tab[idx]
        v = v_tab[idx]
        a = jax.nn.gelu(jnp.einsum('cd,chkd->chk', xc, u), approximate=False)
        w = (g * a.astype(jnp.float32)).astype(x.dtype)
        return jnp.einsum('chk,chkd->cd', w, v)

    y = lax.map(chunk, xt)
    return y.reshape(B, S, D)


def encoder_layer(x, mem, w_in, sink, conv_w, w_mem_k, w_mem_v, w_attn_o, w_conv_out, w_mem_o,
                  w_out, ln1_g, ln1_b, w_peer_q, peer_keys, peer_u, peer_v, ln2_g, ln2_b):
    B, S, D = x.shape
    alpha = (2.0 * DEPTH) ** 0.25
    proj = x @ w_in
    cuts = list(np.cumsum(SPLIT_SIZES)[:-1])
    q, k, v, ch, cb, cc, mq, gl = jnp.split(proj, cuts, axis=-1)
    attn = windowed_gqa(q.reshape(B, S, N_Q_HEADS, HEAD_DIM),
                        k.reshape(B, S, N_KV_HEADS, HEAD_DIM),
                        v.reshape(B, S, N_KV_HEADS, HEAD_DIM), sink) @ w_attn_o
    conv = short_gated_conv(ch, cb, cc, conv_w) @ w_conv_out
    M = mem.shape[1]
    mk = (mem @ w_mem_k).reshape(B, M, MEM_HEADS, MEM_HEAD_DIM)
    mv = (mem @ w_mem_v).reshape(B, M, MEM_HEADS, MEM_HEAD_DIM)
    memo = memory_cross_attn(mq.reshape(B, S, MEM_HEADS, MEM_HEAD_DIM), mk, mv) @ w_mem_o
    gates = jax.nn.sigmoid(gl).reshape(B, S, N_BRANCHES, D)
    merged = gates[:, :, 0] * attn + gates[:, :, 1] * conv + gates[:, :, 2] * memo
    h = layer_norm(alpha * x + merged @ w_out, ln1_g, ln1_b)
    return layer_norm(alpha * h + peer_ffn(h, w_peer_q, peer_keys, peer_u, peer_v), ln2_g, ln2_b)


def setup_inputs(seed: int = 0) -> dict:
    key = jax.random.key(seed)
    ks = jax.random.split(key, 24)
    L = DEPTH
    beta = (8.0 * DEPTH) ** -0.25

    def nrm(k, shape, scale):
        return jax.random.normal(k, shape, jnp.float32) * scale

    return {
        'x_prompt': nrm(ks[0], (BATCH, SEQ, D_MODEL), 1.0),
        'x_sample': nrm(ks[1], (DEC_BATCH, DEC_SEQ, D_MODEL), 1.0),
        'mem_prompt': nrm(ks[2], (BATCH, N_MEM, D_MODEL), 1.0),
        'mem_sample': nrm(ks[3], (DEC_BATCH, N_MEM, D_MODEL), 1.0),
        'w_in': nrm(ks[4], (L, D_MODEL, IN_WIDTH), D_MODEL ** -0.5),
        'sink': nrm(ks[5], (L, N_Q_HEADS), 0.5),
        'conv_w': nrm(ks[6], (L, CONV_K, CONV_WIDTH), CONV_K ** -0.5),
        'w_mem_k': nrm(ks[7], (L, D_MODEL, MEM_WIDTH), D_MODEL ** -0.5),
        'w_mem_v': nrm(ks[8], (L, D_MODEL, MEM_WIDTH), D_MODEL ** -0.5),
        'w_attn_o': nrm(ks[9], (L, ATTN_WIDTH, D_MODEL), beta * ATTN_WIDTH ** -0.5),
        'w_conv_out': nrm(ks[10], (L, CONV_WIDTH, D_MODEL), beta * CONV_WIDTH ** -0.5),
        'w_mem_o': nrm(ks[11], (L, MEM_WIDTH, D_MODEL), beta * MEM_WIDTH ** -0.5),
        'w_out': nrm(ks[12], (L, D_MODEL, D_MODEL), beta * D_MODEL ** -0.5),
        'ln1_g': 1.0 + nrm(ks[13], (L, D_MODEL), 0.02),
        'ln1_b': nrm(ks[14], (L, D_MODEL), 0.02),
        'w_peer_q': nrm(ks[15], (L, D_MODEL, PEER_HEADS * PEER_QDIM), D_MODEL ** -0.5),
        'peer_keys': nrm(ks[16], (L, PEER_HEADS, 2, N_KEYS, PEER_HALF), PEER_HALF ** -0.5),
        'peer_u': nrm(ks[17], (L, N_EXPERTS, D_MODEL), D_MODEL ** -0.5),
        'peer_v': nrm(ks[18], (L, N_EXPERTS, D_MODEL), beta * PEER_HEADS ** -0.5),
        'ln2_g': 1.0 + nrm(ks[19], (L, D_MODEL), 0.02),
        'ln2_b': nrm(ks[20], (L, D_MODEL), 0.02),
    }


def reference(x_prompt, x_sample, mem_prompt, mem_sample, w_in, sink, conv_w, w_mem_k, w_mem_v,
              w_attn_o, w_conv_out, w_mem_o, w_out, ln1_g, ln1_b, w_peer_q, peer_keys, peer_u,
              peer_v, ln2_g, ln2_b):
    y_prompt = x_prompt
    y_sample = x_sample
    for l in range(DEPTH):
        params = (w_in[l], sink[l], conv_w[l], w_mem_k[l], w_mem_v[l], w_attn_o[l], w_conv_out[l],
                  w_mem_o[l], w_out[l], ln1_g[l], ln1_b[l], w_peer_q[l], peer_keys[l], peer_u[l],
                  peer_v[l], ln2_g[l], ln2_b[l])
        y_prompt = encoder_layer(y_prompt, mem_prompt, *params)
        y_sample = encoder_layer(y_sample, mem_sample, *params)
    return (y_prompt, y_sample)
```

```python
import math
from contextlib import ExitStack

import numpy as np
import concourse.bass as bass
import concourse.mybir as mybir
from concourse.bass_utils import run_bass_kernel_spmd

F32 = mybir.dt.float32
BF16 = mybir.dt.bfloat16
I32 = mybir.dt.int32
U32 = mybir.dt.uint32
AF = mybir.ActivationFunctionType
ALU = mybir.AluOpType

D = 2048
NCORES = 8
SEG = 2048
IN_WIDTH = 11776
ALPHA = 2.0 ** 0.25
LN_EPS = 1e-5
NEG_BIG = 1.0e6


class Buf:
    def __init__(self, name, t, space):
        self.name = name
        self.t = t
        self.space = space
        self.last_w = None
        self.readers = []
        self.dsem = None
        self.dcnt = 0

    def __getitem__(self, k):
        return self.t[k]


class FW:
    ENGS = ("pe", "act", "dve", "pool", "sp")

    def __init__(self, nc):
        self.nc = nc
        self.st = ExitStack()
        self.ops = {e: [] for e in self.ENGS}
        self.cnt = {e: 0 for e in self.ENGS}
        self.cur_sem = {e: None for e in self.ENGS}
        self.seen = {e: {} for e in self.ENGS}
        self.nsem = 0
        self.dma_bufs = []
        self.uid = 0

    def _sem(self, name):
        self.nsem += 1
        return self.st.enter_context(self.nc.semaphore(name))

    def sbuf(self, name, shape, dt, st=None):
        self.uid += 1
        t = (st or self.st).enter_context(self.nc.sbuf_tensor(f"{name}_{self.uid}", list(shape), dt))
        return Buf(name, t, "sb")

    def psum(self, name, shape, dt=F32):
        t = self.st.enter_context(self.nc.psum_tensor(name, list(shape), dt))
        return Buf(name, t, "ps")

    def dram(self, name, shape, dt, kind="Internal"):
        t = self.nc.dram_tensor(name, list(shape), dt, kind=kind)
        return Buf(name, t, "dr")

    def _collect(self, eng, reads, writes):
        waits = {}

        def add(tok):
            if tok is None:
                return
            s, v = tok
            if eng == "pe" and s is self.cur_sem["pe"]:
                return
            if waits.get(s, 0) < v:
                waits[s] = v

        for b in reads:
            add(b.last_w)
        for b in writes:
            add(b.last_w)
            for r in b.readers:
                add(r)
        out = []
        seen = self.seen[eng]
        for s, v in waits.items():
            if seen.get(s, 0) >= v:
                continue
            seen[s] = v
            out.append((s, v))
        return out

    def _commit(self, tok, reads, writes):
        for b in writes:
            b.last_w = tok
            b.readers = []
        for b in reads:
            if b not in writes:
                b.readers.append(tok)
                if len(b.readers) > 64:
                    best = {}
                    for s, v in b.readers:
                        if best.get(s, 0) < v:
                            best[s] = v
                    b.readers = list(best.items())

    def op(self, eng, fn, reads=(), writes=()):
        reads = [b for b in reads if b is not None]
        writes = [b for b in writes if b is not None]
        if self.cur_sem[eng] is None:
            self.cur_sem[eng] = self._sem(f"s_{eng}")
        waits = self._collect(eng, reads, writes)
        self.cnt[eng] += 1
        tok = (self.cur_sem[eng], self.cnt[eng])
        self.ops[eng].append((waits, fn, tok[0], 1))
        self._commit(tok, reads, writes)
        return tok

    def dma(self, eng, out_b, out_ap, in_b, in_ap):
        owner = out_b if out_b.space == "sb" else (in_b if in_b.space == "sb" else out_b)
        if owner.dsem is None:
            owner.dsem = self._sem(f"d_{owner.name}")
            self.dma_bufs.append(owner)
        waits = self._collect(eng, [in_b], [out_b])
        owner.dcnt += 16
        tok = (owner.dsem, owner.dcnt)

        def fn(e, out_ap=out_ap, in_ap=in_ap):
            return e.dma_start(out=out_ap, in_=in_ap)

        self.ops[eng].append((waits, fn, tok[0], 16))
        self._commit(tok, [in_b], [out_b])
        return tok

    def barrier(self):
        toks = [(b.dsem, b.dcnt) for b in self.dma_bufs]
        for e in self.ENGS:
            if self.cur_sem[e] is not None:
                toks.append((self.cur_sem[e], self.cnt[e]))
        for e in self.ENGS:
            seen = self.seen[e]
            w = []
            for s, v in toks:
                if e == "pe" and s is self.cur_sem["pe"]:
                    continue
                if seen.get(s, 0) >= v:
                    continue
                seen[s] = v
                w.append((s, v))
            if w:
                self.ops[e].append((w, None, None, 0))

    def final_tokens(self):
        return [(b.dsem, b.dcnt) for b in self.dma_bufs]

    def emit(self):
        nc = self.nc
        ops = self.ops
        final = self.final_tokens()
        for e in self.ENGS:
            if self.cur_sem[e] is not None and e != "sp":
                final.append((self.cur_sem[e], self.cnt[e]))
        with nc.Block() as block:
            def run(e, lst, fin=None):
                for waits, fn, sem, inc in lst:
                    for s, v in waits:
                        e.wait_ge(s, v)
                    if fn is not None:
                        fn(e).then_inc(sem, inc)
                if fin:
                    for s, v in fin:
                        e.wait_ge(s, v)

            @block.tensor
            def _(e):
                run(e, ops["pe"])

            @block.scalar
            def _(e):
                run(e, ops["act"])

            @block.vector
            def _(e):
                run(e, ops["dve"])

            @block.gpsimd
            def _(e):
                run(e, ops["pool"])

            @block.sync
            def _(e):
                run(e, ops["sp"], final)

    def close(self):
        self.st.close()


def build_program(n_seg=3, seg_tiles=16, debug_h=False):
    nc = bass.Bass("TRN2", target_bir_lowering=False)
    fw = FW(nc)
    NT = n_seg * seg_tiles
    HT = seg_tiles + 2
    assert NT % 2 == 0

    def ein(name, shape, dt=F32):
        return fw.dram(name, shape, dt, kind="ExternalInput")

    xc = ein("xc", [n_seg, HT * 128, D])
    memc = ein("memc", [n_seg, 256, D])
    hval = ein("hval", [128, n_seg * 2 * 128])
    c_ident = ein("c_ident", [128, 128])
    c_dist = ein("c_dist", [128, 3 * 128])
    c_iota = ein("c_iota", [128, 128])
    w_in = ein("w_in", [D, IN_WIDTH])
    sink = ein("sink", [1, 8])
    conv_w = ein("conv_w", [128, 24])
    w_mem_k = ein("w_mem_k", [D, 1024])
    w_mem_v = ein("w_mem_v", [D, 1024])
    w_attn_o = ein("w_attn_o", [1024, D])
    w_conv_out = ein("w_conv_out", [1024, D])
    w_mem_o = ein("w_mem_o", [1024, D])
    w_out = ein("w_out", [D, D])
    ln1_g = ein("ln1_g", [1, D])
    ln1_b = ein("ln1_b", [1, D])
    w_peer_q = ein("w_peer_q", [D, D])
    peer_keys = ein("peer_keys", [16, 128, 128])
    peer_u = ein("peer_u", [16384, D])
    peer_v = ein("peer_v", [16384, D])
    ln2_g = ein("ln2_g", [1, D])
    ln2_b = ein("ln2_b", [1, D])
    yout = fw.dram("yout", [NT * 128, D], F32, kind="ExternalOutput")

    s_wp = fw.dram("s_wp", [22, 128, 4096], BF16)
    s_wg = fw.dram("s_wg", [4, 3, 2, 128, 4096], BF16)
    s_wo = fw.dram("s_wo", [4, 3, 128, 4096], BF16)
    s_wout = fw.dram("s_wout", [8, 128, 4096], BF16)
    s_wq = fw.dram("s_wq", [16, 128, 2048], BF16)
    s_wmk = fw.dram("s_wmk", [4, 128, 4096], BF16)
    s_wmv = fw.dram("s_wmv", [4, 128, 4096], BF16)
    s_uT = fw.dram("s_uT", [128, 128, 2048], BF16)
    s_v = fw.dram("s_v", [128, 128, 2048], BF16)
    s_h = fw.dram("s_h", [NT, 128, D], F32)
    s_hT = fw.dram("s_hT", [NT, 128, D], BF16)

    banks = [fw.psum(f"bank{i}", [128, 512], F32) for i in range(8)]
    bank_i = [0]

    bank_set = [list(range(8))]

    def bank():
        bs = bank_set[0]
        b = banks[bs[bank_i[0] % len(bs)]]
        bank_i[0] += 1
        return b

    ev_i = [0]

    def evac(out_b, out_ap, in_b, in_ap, scale=None, eng=None):
        if eng is None:
            eng = "act" if (ev_i[0] % 2 == 0) else "dve"
            ev_i[0] += 1
        if eng == "act":
            if scale is None:
                fw.op("act", lambda e: e.copy(out_ap, in_ap), [in_b], [out_b])
            else:
                fw.op("act", lambda e: e.mul(out_ap, in_ap, float(scale)), [in_b], [out_b])
        else:
            if scale is None:
                fw.op("dve", lambda e: e.tensor_copy(out_ap, in_ap), [in_b], [out_b])
            else:
                fw.op("dve", lambda e: e.tensor_scalar_mul(out_ap, in_ap, float(scale)), [in_b], [out_b])

    def mm(out_b, out_ap, l_b, l_ap, r_b, r_ap, start, stop):
        fw.op("pe", lambda e: e.matmul(out_ap, l_ap, r_ap, start=start, stop=stop), [l_b, r_b], [out_b])

    cst = fw.st
    identf = fw.sbuf("identf", [128, 128], F32)
    identb = fw.sbuf("identb", [128, 128], BF16)
    onesb = fw.sbuf("onesb", [128, 128], BF16)
    fw.dma("sp", identf, identf[:, :], c_ident, c_ident.t.ap())
    fw.op("dve", lambda e: e.tensor_copy(identb[:, :], identf[:, :]), [identf], [identb])
    fw.op("dve", lambda e: e.memset(onesb[:, :], 1.0), [], [onesb])

    def transpose_bf(out_b, out_ap, in_b, in_ap):
        fw.op("pe", lambda e: e.transpose(out_ap, in_ap, identb[:, :]), [in_b, identb], [out_b])

    def transpose_f(out_b, out_ap, in_b, in_ap):
        fw.op("pe", lambda e: e.transpose(out_ap, in_ap, identf[:, :]), [in_b, identf], [out_b])

    with ExitStack() as ps:
        stg = [fw.sbuf(f"pp_f{i}", [128, 4096], F32, ps) for i in range(3)]
        stb = [fw.sbuf(f"pp_b{i}", [128, 4096], BF16, ps) for i in range(3)]
        ust = [fw.sbuf(f"pp_u{i}", [128, 2048], BF16, ps) for i in range(2)]
        step = [0]
        cast_engs = ["act", "dve", "pool"]

        def convert(src_b, src_ap, a, b_, dst_b, dst_ap):
            n = a * b_
            i = step[0] % 3
            ce = cast_engs[step[0] % 3]
            step[0] += 1
            f, bb = stg[i], stb[i]
            fw.dma("sp", f, f[:, 0:n].rearrange("p (a b) -> p a b", a=a), src_b, src_ap)
            if ce == "act":
                fw.op("act", lambda e: e.copy(bb[:, 0:n], f[:, 0:n]), [f], [bb])
            elif ce == "dve":
                fw.op("dve", lambda e: e.tensor_copy(bb[:, 0:n], f[:, 0:n]), [f], [bb])
            else:
                fw.op("pool", lambda e: e.tensor_copy(bb[:, 0:n], f[:, 0:n]), [f], [bb])
            fw.dma("act", dst_b, dst_ap, bb, bb[:, 0:n])
            return bb

        def wview(wb, kc):
            return wb.t.ap().rearrange("(kc p) n -> p kc n", p=128)

        win_v = wview(w_in, 16)
        for pc in range(22):
            convert(w_in, win_v[:, :, pc * 256:(pc + 1) * 256], 16, 256, s_wp, s_wp.t.ap()[pc])
        for nb in range(4):
            for br in range(3):
                for hf in range(2):
                    c0 = 5632 + br * 2048 + nb * 512 + hf * 256
                    convert(w_in, win_v[:, :, c0:c0 + 256], 16, 256, s_wg, s_wg.t.ap()[nb, br, hf])
        for br, wb in enumerate((w_attn_o, w_conv_out, w_mem_o)):
            v = wview(wb, 8)
            for nb in range(4):
                convert(wb, v[:, :, nb * 512:(nb + 1) * 512], 8, 512, s_wo, s_wo.t.ap()[nb, br])
        v = wview(w_out, 16)
        for pc in range(8):
            convert(w_out, v[:, :, pc * 256:(pc + 1) * 256], 16, 256, s_wout, s_wout.t.ap()[pc])
        v = wview(w_peer_q, 16)
        for hp in range(16):
            convert(w_peer_q, v[:, :, hp * 128:(hp + 1) * 128], 16, 128, s_wq, s_wq.t.ap()[hp])
        v = wview(w_mem_k, 16)
        for pc in range(4):
            convert(w_mem_k, v[:, :, pc * 256:(pc + 1) * 256], 16, 256, s_wmk, s_wmk.t.ap()[pc])
        v = wview(w_mem_v, 16)
        for pc in range(4):
            convert(w_mem_v, v[:, :, pc * 256:(pc + 1) * 256], 16, 256, s_wmv, s_wmv.t.ap()[pc])
        vv = peer_v.t.ap().rearrange("(i j) d -> j i d", j=128)
        uu = peer_u.t.ap().rearrange("(i j) d -> j i d", j=128)
        for j in range(128):
            convert(peer_v, vv[j].rearrange("p (a b) -> p a b", a=1), 1, 2048, s_v, s_v.t.ap()[j])
        for j in range(128):
            n = 2048
            i = step[0] % 3
            ce = cast_engs[step[0] % 3]
            step[0] += 1
            f, bb = stg[i], stb[i]
            fw.dma("sp", f, f[:, 0:n], peer_u, uu[j])
            if ce == "act":
                fw.op("act", lambda e, bb=bb, f=f: e.copy(bb[:, 0:n], f[:, 0:n]), [f], [bb])
            elif ce == "dve":
                fw.op("dve", lambda e, bb=bb, f=f: e.tensor_copy(bb[:, 0:n], f[:, 0:n]), [f], [bb])
            else:
                fw.op("pool", lambda e, bb=bb, f=f: e.tensor_copy(bb[:, 0:n], f[:, 0:n]), [f], [bb])
            us = ust[j % 2]
            for half in range(2):
                pb = bank()
                pv = pb.t[:, :].bitcast(BF16)
                for k in range(8):
                    dk = half * 8 + k
                    transpose_bf(pb, pv[:, k * 128:(k + 1) * 128], bb, bb[:, dk * 128:(dk + 1) * 128])
                evac(us, us[:, half * 1024:(half + 1) * 1024], pb, pv[:, :], eng="act" if half == 0 else "dve")
            fw.dma("act", s_uT, s_uT.t.ap()[j], us, us[:, :])
    fw.barrier()

    with ExitStack() as pa:
        def S(name, shape, dt):
            return fw.sbuf(name, shape, dt, pa)

        dist = S("dist", [128, 3, 128], F32)
        fw.dma("sp", dist, dist[:, :, :].rearrange("p a b -> p (a b)"), c_dist, c_dist.t.ap())
        hvb = S("hvb", [128, n_seg * 2, 128], BF16)
        esk = S("esk", [128, 8], F32)
        fw.dma("sp", esk, esk[:, :], sink, sink.t.ap().broadcast_to([128, 8]))
        fw.op("act", lambda e: e.activation(esk[:, :], esk[:, :], AF.Exp), [esk], [esk])
        cwt = S("cwt", [128, 3, 8], F32)
        fw.dma("sp", cwt, cwt[:, :, :].rearrange("p a b -> p (a b)"), conv_w, conv_w.t.ap())
        R = 4
        xT = S("xT", [128, R, 16, 128], BF16)
        kT = S("kT", [128, R, 2, 128], BF16)
        vr = S("vr", [128, R, 256], BF16)
        zT = S("zT", [128, R, 8, 128], BF16)
        Q3 = 3
        qT = [S(f"qT{i}", [128, 8, 128], BF16) for i in range(Q3)]
        mqT = [S(f"mqT{i}", [128, 8, 128], BF16) for i in range(Q3)]
        cbT = [S(f"cbT{i}", [128, 8, 128], BF16) for i in range(Q3)]
        xs = [S(f"xs{i}", [128, D], F32) for i in range(2)]
        NHV = n_seg * 2 * 128
        fw.dma("sp", xs[0], xs[0][:, 0:NHV], hval, hval.t.ap())
        fw.op("dve", lambda e: e.tensor_copy(hvb[:, :, :].rearrange("p a b -> p (a b)"), xs[0][:, 0:NHV]), [xs[0]], [hvb])
        qtm = [S(f"qtm{i}", [128, 1024], BF16) for i in range(2)]
        ktm = [S(f"ktm{i}", [128, 256], BF16) for i in range(2)]
        chs = [S(f"chs{i}", [128, 1024], F32) for i in range(2)]
        ztm = [S(f"ztm{i}", [128, 1024], BF16) for i in range(2)]
        cbtm = [S(f"cbtm{i}", [128, 1024], BF16) for i in range(2)]
        mqtm = [S(f"mqtm{i}", [128, 1024], BF16) for i in range(2)]
        tmpS = [S(f"tmpS{i}", [128, 512], F32) for i in range(2)]
        PT = [S(f"PT{i}", [128, 3, 512], BF16) for i in range(2)]
        rden = S("rden", [128, 512], F32)
        attnT = [S(f"attnT{i}", [128, 8, 128], BF16) for i in range(2)]
        convT = [S(f"convT{i}", [128, 8, 128], BF16) for i in range(2)]
        memoT = [S(f"memoT{i}", [128, 8, 128], BF16) for i in range(2)]
        cacc = S("cacc", [128, 8, 128], F32)
        ctmp = S("ctmp", [128, 8, 128], F32)
        PTm = [S(f"PTm{i}", [128, 2, 128], BF16) for i in range(2)]
        rdm = S("rdm", [128, 128], F32)
        mkT = S("mkT", [128, 8, 256], BF16)
        mv = S("mv", [128, 2, 1024], BF16)
        gsb = [S(f"gsb{i}", [128, 3, 512], BF16) for i in range(2)]
        mtmp = [S(f"mtmp{i}", [128, 512], F32) for i in range(2)]
        mgtm = [S(f"mgtm{i}", [128, D], BF16) for i in range(2)]
        mgT = [S(f"mgT{i}", [128, 16, 128], BF16) for i in range(2)]
        hbuf = [S(f"hbuf{i}", [128, D], F32) for i in range(2)]
        memT = hbuf[0]
        memT_v = hbuf[0].t[:, :].bitcast(BF16).rearrange("p (a b) -> p a b", a=16)
        stats = S("stats", [128, 4, 6], F32)
        mvar = S("mvar", [128, 2], F32)
        rstd = S("rstd", [128, 1], F32)
        NW = 3
        wring = [S(f"wr{i}", [128, 4096], BF16) for i in range(NW)]
        wr_i = [0]

        def wload(src_b, src_ap):
            w = wring[wr_i[0] % NW]
            wr_i[0] += 1
            fw.dma("sp", w, w[:, :], src_b, src_ap)
            return w

        slopes = [2.0 ** (-(h + 1)) for h in range(8)]

        def load_xT(seg, ht):
            x_s = xs[ht % 2]
            fw.dma("sp", x_s, x_s[:, :], xc, xc.t.ap()[seg, ht * 128:(ht + 1) * 128, :])
            sl = ht % R
            for q4 in range(4):
                pb = bank()
                for k in range(4):
                    dk = q4 * 4 + k
                    transpose_f(pb, pb[:, k * 128:(k + 1) * 128], x_s, x_s[:, dk * 128:(dk + 1) * 128])
                evac(xT, xT[:, sl, q4 * 4:(q4 + 1) * 4, :].rearrange("p a b -> p (a b)"), pb, pb[:, :])

        def proj_block(hts, pc):
            w = wload(s_wp, s_wp.t.ap()[pc])
            out = []
            for ht in hts:
                sl = ht % R
                pb = bank()
                for dk in range(16):
                    mm(pb, pb[:, 0:256], xT, xT[:, sl, dk, :], w, w[:, dk * 256:(dk + 1) * 256], dk == 0, dk == 15)
                out.append(pb)
            return out

        def tr_group(dst_b, dst_ap, src_b, src, n):
            pb = bank()
            pv = pb.t[:, :].bitcast(BF16)
            for k in range(n):
                transpose_bf(pb, pv[:, k * 128:(k + 1) * 128], src_b, src[:, k * 128:(k + 1) * 128])
            evac(dst_b, dst_ap, pb, pv[:, 0:n * 128])

        def stage_P(seg, hts, fulls):
            fl = [ht for ht, f in zip(hts, fulls) if f]
            idx = {ht: i for i, ht in enumerate(hts)}
            if fl:
                for pc in range(4):
                    for ht, pb in zip(fl, proj_block(fl, pc)):
                        evac(qtm[idx[ht]], qtm[idx[ht]][:, pc * 256:(pc + 1) * 256], pb, pb[:, 0:256], scale=1.0 / math.sqrt(128.0))
            for ht, pb in zip(hts, proj_block(hts, 4)):
                evac(ktm[idx[ht]], ktm[idx[ht]][:, :], pb, pb[:, 0:256])
            for ht, pb in zip(hts, proj_block(hts, 5)):
                evac(vr, vr[:, ht % R, :], pb, pb[:, 0:256])
            for pc in range(4):
                for ht, pb in zip(hts, proj_block(hts, 6 + pc)):
                    evac(chs[idx[ht]], chs[idx[ht]][:, pc * 256:(pc + 1) * 256], pb, pb[:, 0:256])
            if fl:
                for pc in range(4):
                    for ht, pb in zip(fl, proj_block(fl, 10 + pc)):
                        evac(cbtm[idx[ht]], cbtm[idx[ht]][:, pc * 256:(pc + 1) * 256], pb, pb[:, 0:256])
            for pc in range(4):
                for ht, pb in zip(hts, proj_block(hts, 14 + pc)):
                    i = idx[ht]
                    fw.op("dve", lambda e, pb=pb, pc=pc, i=i: e.tensor_tensor(
                        ztm[i][:, pc * 256:(pc + 1) * 256], pb[:, 0:256], chs[i][:, pc * 256:(pc + 1) * 256], ALU.mult),
                        [pb, chs[i]], [ztm[i]])
            if fl:
                for pc in range(4):
                    for ht, pb in zip(fl, proj_block(fl, 18 + pc)):
                        evac(mqtm[idx[ht]], mqtm[idx[ht]][:, pc * 256:(pc + 1) * 256], pb, pb[:, 0:256], scale=1.0 / 16.0)
            for ht, f in zip(hts, fulls):
                i = idx[ht]
                sl = ht % R
                tr_group(kT, kT[:, sl, :, :].rearrange("p a b -> p (a b)"), ktm[i], ktm[i], 2)
                tr_group(zT, zT[:, sl, :, :].rearrange("p a b -> p (a b)"), ztm[i], ztm[i], 8)
                if f:
                    q3 = ht % Q3
                    tr_group(qT[q3], qT[q3][:, :, :].rearrange("p a b -> p (a b)"), qtm[i], qtm[i], 8)
                    tr_group(cbT[q3], cbT[q3][:, :, :].rearrange("p a b -> p (a b)"), cbtm[i], cbtm[i], 8)
                    tr_group(mqT[q3], mqT[q3][:, :, :].rearrange("p a b -> p (a b)"), mqtm[i], mqtm[i], 8)

        def seg_memory(seg):
            for mt in range(2):
                x_s = xs[mt]
                fw.dma("sp", x_s, x_s[:, :], memc, memc.t.ap()[seg, mt * 128:(mt + 1) * 128, :])
                for q4 in range(4):
                    pb = bank()
                    for k in range(4):
                        dk = q4 * 4 + k
                        transpose_f(pb, pb[:, k * 128:(k + 1) * 128], x_s, x_s[:, dk * 128:(dk + 1) * 128])
                    evac(memT, memT_v[:, q4 * 4:(q4 + 1) * 4, mt * 128:(mt + 1) * 128],
                         pb, pb[:, :].rearrange("p (a b) -> p a b", a=4))
            for pc in range(4):
                w = wload(s_wmk, s_wmk.t.ap()[pc])
                for c2 in range(2):
                    pb = bank()
                    for dk in range(16):
                        mm(pb, pb[:, 0:256], w, w[:, dk * 256 + c2 * 128: dk * 256 + c2 * 128 + 128],
                           memT, memT_v[:, dk, :], dk == 0, dk == 15)
                    evac(mkT, mkT[:, pc * 2 + c2, :], pb, pb[:, 0:256])
            for pc in range(4):
                w = wload(s_wmv, s_wmv.t.ap()[pc])
                for mt in range(2):
                    pb = bank()
                    for dk in range(16):
                        mm(pb, pb[:, 0:256], memT, memT_v[:, dk, mt * 128:(mt + 1) * 128],
                           w, w[:, dk * 256:(dk + 1) * 256], dk == 0, dk == 15)
                    evac(mv, mv[:, mt, pc * 256:(pc + 1) * 256], pb, pb[:, 0:256])

        def mixers(seg, ht, ti):
            sl = ht % R
            q3 = ht % Q3
            q_t, mq_t, cb_t = qT[q3], mqT[q3], cbT[q3]
            aT, cT, mT = attnT[ti], convT[ti], memoT[ti]
            for kvh in range(2):
                pt = PT[kvh]
                for bi, bo in enumerate((-1, 0, 1)):
                    ks = (ht + bo) % R
                    pb = bank()
                    mm(pb, pb[:, :], kT, kT[:, ks, kvh, :], q_t, q_t[:, kvh * 4:(kvh + 1) * 4, :], True, True)
                    tm = tmpS[bi % 2]
                    for hh in range(4):
                        h = kvh * 4 + hh
                        fw.op("dve", lambda e, pb=pb, tm=tm, hh=hh, h=h, bi=bi: e.scalar_tensor_tensor(
                            tm[:, hh * 128:(hh + 1) * 128], dist[:, bi, :], -slopes[h], pb[:, hh * 128:(hh + 1) * 128],
                            ALU.mult, ALU.add), [pb, dist], [tm])
                    fw.op("act", lambda e, tm=tm, pt=pt, bi=bi: e.activation(pt[:, bi, :], tm[:, :], AF.Exp), [tm], [pt])
                po = bank()
                pd = bank()
                for bi, bo in enumerate((-1, 0, 1)):
                    ks = (ht + bo) % R
                    mm(po, po[:, :], vr, vr[:, ks, kvh * 128:(kvh + 1) * 128], pt, pt[:, bi, :], bi == 0, bi == 2)
                for bi, bo in enumerate((-1, 0, 1)):
                    t_abs = ht + bo
                    if t_abs == 0:
                        vb, va = hvb, hvb[:, seg * 2 + 0, :]
                    elif t_abs == HT - 1:
                        vb, va = hvb, hvb[:, seg * 2 + 1, :]
                    else:
                        vb, va = onesb, onesb[:, :]
                    mm(pd, pd[:, :], vb, va, pt, pt[:, bi, :], bi == 0, bi == 2)
                for hh in range(4):
                    h = kvh * 4 + hh
                    fw.op("dve", lambda e, pd=pd, hh=hh, h=h: e.tensor_scalar(
                        rden[:, hh * 128:(hh + 1) * 128], pd[:, hh * 128:(hh + 1) * 128], esk[:, h:h + 1], None,
                        ALU.add), [pd, esk], [rden])
                fw.op("dve", lambda e: e.reciprocal(rden[:, :], rden[:, :]), [rden], [rden])
                fw.op("dve", lambda e, po=po, kvh=kvh, aT=aT: e.tensor_tensor(
                    aT[:, kvh * 4:(kvh + 1) * 4, :].rearrange("p a b -> p (a b)"), po[:, :], rden[:, :], ALU.mult),
                    [po, rden], [aT])
            sp_, sn_ = (ht - 1) % R, (ht + 1) % R

            def cw(k, n):
                return cwt[:, k, :].unsqueeze(2).broadcast_to([128, 8, n])

            P = "pool"
            fw.op(P, lambda e: e.tensor_tensor(cacc[:, :, :], zT[:, sl, :, :], cw(1, 128), ALU.mult), [zT, cwt], [cacc])
            fw.op(P, lambda e: e.tensor_tensor(ctmp[:, :, 1:128], zT[:, sl, :, 0:127], cw(0, 127), ALU.mult), [zT, cwt], [ctmp])
            fw.op(P, lambda e: e.tensor_tensor(ctmp[:, :, 0:1], zT[:, sp_, :, 127:128], cw(0, 1), ALU.mult), [zT, cwt], [ctmp])
            fw.op(P, lambda e: e.tensor_tensor(cacc[:, :, :], cacc[:, :, :], ctmp[:, :, :], ALU.add), [ctmp, cacc], [cacc])
            fw.op(P, lambda e: e.tensor_tensor(ctmp[:, :, 0:127], zT[:, sl, :, 1:128], cw(2, 127), ALU.mult), [zT, cwt], [ctmp])
            fw.op(P, lambda e: e.tensor_tensor(ctmp[:, :, 127:128], zT[:, sn_, :, 0:1], cw(2, 1), ALU.mult), [zT, cwt], [ctmp])
            fw.op(P, lambda e: e.tensor_tensor(cacc[:, :, :], cacc[:, :, :], ctmp[:, :, :], ALU.add), [ctmp, cacc], [cacc])
            fw.op(P, lambda e: e.tensor_tensor(cT[:, :, :], cacc[:, :, :], cb_t[:, :, :], ALU.mult), [cacc, cb_t], [cT])
            for h in range(4):
                ptm = PTm[h % 2]
                for mc in range(2):
                    pb = bank()
                    for dh in range(2):
                        mm(pb, pb[:, 0:128], mkT, mkT[:, 2 * h + dh, mc * 128:(mc + 1) * 128],
                           mq_t, mq_t[:, 2 * h + dh, :], dh == 0, dh == 1)
                    fw.op("act", lambda e, pb=pb, ptm=ptm, mc=mc: e.activation(ptm[:, mc, :], pb[:, 0:128], AF.Exp), [pb], [ptm])
                pd = bank()
                for mc in range(2):
                    mm(pd, pd[:, 0:128], onesb, onesb[:, :], ptm, ptm[:, mc, :], mc == 0, mc == 1)
                fw.op("dve", lambda e, pd=pd: e.reciprocal(rdm[:, :], pd[:, 0:128]), [pd], [rdm])
                for dh in range(2):
                    po = bank()
                    for mc in range(2):
                        mm(po, po[:, 0:128], mv, mv[:, mc, h * 256 + dh * 128: h * 256 + dh * 128 + 128],
                           ptm, ptm[:, mc, :], mc == 0, mc == 1)
                    fw.op("dve", lambda e, po=po, h=h, dh=dh, mT=mT: e.tensor_tensor(
                        mT[:, 2 * h + dh, :], po[:, 0:128], rdm[:, :], ALU.mult), [po, rdm], [mT])

        def stage_A(seg, hts, gtiles):
            nt = len(hts)
            for ti, ht in enumerate(hts):
                mixers(seg, ht, ti)
            for nb in range(4):
                for br in range(3):
                    pgs = [bank() for _ in range(nt)]
                    for hf in range(2):
                        w = wload(s_wg, s_wg.t.ap()[nb, br, hf])
                        for ti, ht in enumerate(hts):
                            pg = pgs[ti]
                            for dk in range(16):
                                mm(pg, pg[:, hf * 256:(hf + 1) * 256], xT, xT[:, ht % R, dk, :],
                                   w, w[:, dk * 256:(dk + 1) * 256], dk == 0, dk == 15)
                    for ti in range(nt):
                        fw.op("act", lambda e, pg=pgs[ti], g=gsb[ti], br=br: e.activation(g[:, br, :], pg[:, :], AF.Sigmoid), [pgs[ti]], [gsb[ti]])
                pos_ = [[None] * 3 for _ in range(nt)]
                for br in range(3):
                    w = wload(s_wo, s_wo.t.ap()[nb, br])
                    for ti in range(nt):
                        src = (attnT[ti], convT[ti], memoT[ti])[br]
                        po = bank()
                        for kc in range(8):
                            mm(po, po[:, :], src, src[:, kc, :], w, w[:, kc * 512:(kc + 1) * 512], kc == 0, kc == 7)
                        pos_[ti][br] = po
                m0, m1 = mtmp
                for ti in range(nt):
                    g = gsb[ti]
                    p0, p1, p2 = pos_[ti]
                    fw.op("dve", lambda e, po=p0, g=g: e.tensor_tensor(m0[:, :], po[:, :], g[:, 0, :], ALU.mult), [p0, g], [m0])
                    fw.op("dve", lambda e, po=p1, g=g: e.tensor_tensor(m1[:, :], po[:, :], g[:, 1, :], ALU.mult), [p1, g], [m1])
                    fw.op("dve", lambda e: e.tensor_tensor(m0[:, :], m0[:, :], m1[:, :], ALU.add), [m0, m1], [m0])
                    fw.op("dve", lambda e, po=p2, g=g: e.tensor_tensor(m1[:, :], po[:, :], g[:, 2, :], ALU.mult), [p2, g], [m1])
                    fw.op("dve", lambda e, nb=nb, mg=mgtm[ti]: e.tensor_tensor(mg[:, nb * 512:(nb + 1) * 512], m0[:, :], m1[:, :], ALU.add), [m0, m1], [mgtm[ti]])
            for ti in range(nt):
                for half in range(2):
                    tr_group(mgT[ti], mgT[ti][:, half * 8:(half + 1) * 8, :].rearrange("p a b -> p (a b)"),
                             mgtm[ti], mgtm[ti][:, half * 1024:(half + 1) * 1024], 8)
            for ti, ht in enumerate(hts):
                hb = hbuf[ti]
                fw.dma("sp", hb, hb[:, :], xc, xc.t.ap()[seg, ht * 128:(ht + 1) * 128, :])
            for pc in range(8):
                w = wload(s_wout, s_wout.t.ap()[pc])
                for ti in range(nt):
                    hb = hbuf[ti]
                    pb = bank()
                    for dk in range(16):
                        mm(pb, pb[:, 0:256], mgT[ti], mgT[ti][:, dk, :], w, w[:, dk * 256:(dk + 1) * 256], dk == 0, dk == 15)
                    fw.op("dve", lambda e, pb=pb, pc=pc, hb=hb: e.scalar_tensor_tensor(
                        hb[:, pc * 256:(pc + 1) * 256], hb[:, pc * 256:(pc + 1) * 256], ALPHA, pb[:, 0:256],
                        ALU.mult, ALU.add), [pb, hb], [hb])
            g1b = wring[wr_i[0] % NW]
            wr_i[0] += 1
            b1b = wring[wr_i[0] % NW]
            wr_i[0] += 1
            g1v = g1b.t[:, :].bitcast(F32)
            b1v = b1b.t[:, :].bitcast(F32)
            fw.dma("sp", g1b, g1v, ln1_g, ln1_g.t.ap().broadcast_to([128, D]))
            fw.dma("sp", b1b, b1v, ln1_b, ln1_b.t.ap().broadcast_to([128, D]))
            for ti, ht in enumerate(hts):
                hb = hbuf[ti]
                gtile = gtiles[ti]
                layer_norm(hb, g1b, b1b, g1v, b1v)
                hb16 = mgtm[ti]
                hTs = mgT[ti]
                fw.op("act", lambda e, hb=hb, hb16=hb16: e.copy(hb16[:, :], hb[:, :]), [hb], [hb16])
                deferredA.append((s_h, s_h.t.ap()[gtile], hb, hb[:, :]))
                for half in range(2):
                    tr_group(hTs, hTs[:, half * 8:(half + 1) * 8, :].rearrange("p a b -> p (a b)"),
                             hb16, hb16[:, half * 1024:(half + 1) * 1024], 8)
                deferredA.append((s_hT, s_hT.t.ap()[gtile], hTs, hTs[:, :, :].rearrange("p a b -> p (a b)")))

        def layer_norm(hb, gb, bb, gv, bv):
            for c in range(4):
                fw.op("dve", lambda e, c=c: e.bn_stats(stats[:, c, :], hb[:, c * 512:(c + 1) * 512]), [hb], [stats])
            fw.op("dve", lambda e: e.bn_aggr(mvar[:, :], stats[:, :, :].rearrange("p a b -> p (a b)")), [stats], [mvar])
            fw.op("act", lambda e: e.activation(rstd[:, :], mvar[:, 1:2], AF.Sqrt, bias=LN_EPS), [mvar], [rstd])
            fw.op("dve", lambda e: e.reciprocal(rstd[:, :], rstd[:, :]), [rstd], [rstd])
            fw.op("dve", lambda e: e.tensor_scalar(hb[:, :], hb[:, :], mvar[:, 0:1], rstd[:, 0:1], ALU.subtract, ALU.mult),
                  [hb, mvar, rstd], [hb])
            fw.op("pool", lambda e: e.tensor_tensor(hb[:, :], hb[:, :], gv, ALU.mult), [hb, gb], [hb])
            fw.op("pool", lambda e: e.tensor_tensor(hb[:, :], hb[:, :], bv, ALU.add), [hb, bb], [hb])

        deferredA = []

        def flushA():
            for a_ in deferredA:
                fw.dma("sp", *a_)
            deferredA.clear()

        assert seg_tiles % 2 == 0
        for seg in range(n_seg):
            seg_memory(seg)
            load_xT(seg, 0)
            stage_P(seg, [0], [False])
            load_xT(seg, 1)
            stage_P(seg, [1], [True])
            for i in range(seg_tiles // 2):
                a, b = 2 * i + 2, 2 * i + 3
                load_xT(seg, a)
                load_xT(seg, b)
                stage_P(seg, [a, b], [True, b <= seg_tiles])
                flushA()
                stage_A(seg, [2 * i + 1, 2 * i + 2], [seg * seg_tiles + 2 * i, seg * seg_tiles + 2 * i + 1])
            flushA()
    fw.barrier()

    if not debug_h:
        build_phase_B(nc, fw, locals())
    else:
        with ExitStack() as pdbg:
            hb = fw.sbuf("dbg", [128, D], F32, pdbg)
            for gt in range(NT):
                fw.dma("sp", hb, hb[:, :], s_h, s_h.t.ap()[gt])
                fw.dma("sp", yout, yout.t.ap()[gt * 128:(gt + 1) * 128, :], hb, hb[:, :])
    fw.emit()
    fw.close()
    return nc


def build_phase_B(nc, fw, env):
    NT = env["NT"]
    bank = env["bank"]
    bank_set = env["bank_set"]
    banks = env["banks"]
    evac = env["evac"]
    mm = env["mm"]
    transpose_bf = env["transpose_bf"]
    transpose_f = env["transpose_f"]
    identf = env["identf"]
    s_wq, s_uT, s_v, s_h, s_hT, yout = (env[k] for k in ("s_wq", "s_uT", "s_v", "s_h", "s_hT", "yout"))
    peer_keys, ln2_g, ln2_b, c_iota = (env[k] for k in ("peer_keys", "ln2_g", "ln2_b", "c_iota"))
    NG = NT // 2
    with ExitStack() as pb_:
        def S(name, shape, dt):
            return fw.sbuf(name, shape, dt, pb_)

        g2b = S("g2b", [128, D], F32)
        b2b = S("b2b", [128, D], F32)
        fw.dma("sp", g2b, g2b[:, :], ln2_g, ln2_g.t.ap().broadcast_to([128, D]))
        fw.dma("sp", b2b, b2b[:, :], ln2_b, ln2_b.t.ap().broadcast_to([128, D]))
        iota = S("iota", [128, 128], F32)
        fw.dma("sp", iota, iota[:, :], c_iota, c_iota.t.ap())
        keysT = S("keysT", [128, 16, 128], BF16)
        kst = S("kst", [128, 128], F32)
        for hp in range(16):
            fw.dma("sp", kst, kst[:, :], peer_keys, peer_keys.t.ap()[hp])
            pb = bank()
            transpose_f(pb, pb[:, 0:128], kst, kst[:, :])
            evac(keysT, keysT[:, hp, :], pb, pb[:, 0:128])

        hTg = S("hTg", [128, 2, 16, 128], BF16)
        qpT = S("qpT", [128, 16, 256], BF16)
        sc = S("sc", [128, 16, 128], F32)
        T = S("T", [128, 16, 16], F32)
        Ix = S("Ix", [128, 16, 16], U32)
        Ixf = S("Ixf", [128, 16, 16], F32)
        wk = S("wk", [128, 128], F32)
        C = S("C", [128, 16, 16], F32)
        wk2 = S("wk2", [128, 256], F32)
        top = S("top", [128, 8, 16], F32)
        pos = S("pos", [128, 8, 16], U32)
        r1i = S("r1i", [128, 8, 16], I32)
        r2i = S("r2i", [128, 8, 16], I32)
        r1f = S("r1f", [128, 8, 16], F32)
        r2f = S("r2f", [128, 8, 16], F32)
        oh = S("oh", [128, 16, 16], F32)
        Isel = [S(f"Isel{i}", [128, 128], F32) for i in range(2)]
        Jsel = [S(f"Jsel{i}", [128, 128], F32) for i in range(2)]
        gsel = [S(f"gsel{i}", [128, 128], F32) for i in range(2)]
        zsum = S("zsum", [128, 8], F32)
        ITt = S("ITt", [128, 2, 3, 128], F32)
        QT = 16
        ATs = [S(f"AT{i}", [128, QT, 128], BF16) for i in range(2)]
        BTs = [S(f"BT{i}", [128, QT, 128], BF16) for i in range(2)]
        Gall = S("Gall", [128, 128, 256], BF16)
        NS = 4
        uring = [S(f"ur{i}", [128, 2048], BF16) for i in range(NS)]
        NV = 8
        vring = [S(f"vr{i}", [128, 1024], BF16) for i in range(NV)]
        ga = [S(f"ga{i}", [128, 256], BF16) for i in range(2)]
        hbuf = [S(f"hbB{i}", [128, D], F32) for i in range(2)]
        stats = S("statsB", [128, 4, 6], F32)
        mvar = S("mvarB", [128, 2], F32)
        rstd = S("rstdB", [128, 1], F32)
        ur_i = [0]
        vr_i = [0]

        def V(e):
            return e

        def dve(fn, r, w):
            fw.op("dve", fn, r, w)

        def topk_group(g, part):
            if part == 0:
              for tl in range(2):
                fw.dma("sp", hTg, hTg[:, tl, :, :].rearrange("p a b -> p (a b)"), s_hT, s_hT.t.ap()[2 * g + tl])
              for hp in range(16):
                w = uring[ur_i[0] % NS]
                ur_i[0] += 1
                fw.dma("sp", w, w[:, :], s_wq, s_wq.t.ap()[hp])
                pb = bank()
                for dk in range(16):
                    mm(pb, pb[:, 0:256], w, w[:, dk * 128:(dk + 1) * 128], hTg, hTg[:, :, dk, :], dk == 0, dk == 15)
                evac(qpT, qpT[:, hp, :], pb, pb[:, 0:256])
            for tl in (part,):
                for q4 in range(4):
                    pb = bank()
                    for k in range(4):
                        hp = q4 * 4 + k
                        mm(pb, pb[:, k * 128:(k + 1) * 128], qpT, qpT[:, hp, tl * 128:(tl + 1) * 128],
                           keysT, keysT[:, hp, :], True, True)
                    evac(sc, sc[:, q4 * 4:(q4 + 1) * 4, :].rearrange("p a b -> p (a b)"), pb, pb[:, :])
                for hp in range(16):
                    dve(lambda e, hp=hp: e.max(out=T[:, hp, 0:8], in_=sc[:, hp, :]), [sc], [T])
                    dve(lambda e, hp=hp: e.max_index(out=Ix[:, hp, 0:8], in_max=T[:, hp, 0:8], in_values=sc[:, hp, :]), [sc, T], [Ix])
                    dve(lambda e, hp=hp: e.match_replace(out=wk[:, :], in_to_replace=T[:, hp, 0:8], in_values=sc[:, hp, :], imm_value=-1e30), [sc, T], [wk])
                    dve(lambda e, hp=hp: e.max(out=T[:, hp, 8:16], in_=wk[:, :]), [wk], [T])
                    dve(lambda e, hp=hp: e.max_index(out=Ix[:, hp, 8:16], in_max=T[:, hp, 8:16], in_values=wk[:, :]), [wk, T], [Ix])
                dve(lambda e: e.tensor_copy(Ixf[:, :, :], Ix[:, :, :].bitcast(I32)), [Ix], [Ixf])
                for h in range(8):
                    a1 = T[:, 2 * h, :].unsqueeze(2).broadcast_to([128, 16, 16])
                    a2 = T[:, 2 * h + 1, :].unsqueeze(1).broadcast_to([128, 16, 16])
                    Cf = C[:, :, :].rearrange("p a b -> p (a b)")
                    dve(lambda e, a1=a1, a2=a2: e.tensor_tensor(C[:, :, :], a1, a2, ALU.add), [T], [C])
                    dve(lambda e, h=h, Cf=Cf: e.max(out=top[:, h, 0:8], in_=Cf), [C], [top])
                    dve(lambda e, h=h, Cf=Cf: e.max_index(out=pos[:, h, 0:8], in_max=top[:, h, 0:8], in_values=Cf), [C, top], [pos])
                    dve(lambda e, h=h, Cf=Cf: e.match_replace(out=wk2[:, :], in_to_replace=top[:, h, 0:8], in_values=Cf, imm_value=-1e30), [C, top], [wk2])
                    dve(lambda e, h=h: e.max(out=top[:, h, 8:16], in_=wk2[:, :]), [wk2], [top])
                    dve(lambda e, h=h: e.max_index(out=pos[:, h, 8:16], in_max=top[:, h, 8:16], in_values=wk2[:, :]), [wk2, top], [pos])
                posi = pos[:, :, :].bitcast(I32)
                dve(lambda e, posi=posi: e.tensor_single_scalar(out=r1i[:, :, :], in_=posi, scalar=4, op=ALU.arith_shift_right), [pos], [r1i])
                dve(lambda e, posi=posi: e.tensor_single_scalar(out=r2i[:, :, :], in_=posi, scalar=15, op=ALU.bitwise_and), [pos], [r2i])
                dve(lambda e: e.tensor_copy(r1f[:, :, :], r1i[:, :, :]), [r1i], [r1f])
                dve(lambda e: e.tensor_copy(r2f[:, :, :], r2i[:, :, :]), [r2i], [r2f])
                io16 = iota[:, 0:16].unsqueeze(1).broadcast_to([128, 16, 16])
                for h in range(8):
                    for (rf, hp, dst) in ((r1f, 2 * h, Isel[tl]), (r2f, 2 * h + 1, Jsel[tl])):
                        rb = rf[:, h, :].unsqueeze(2).broadcast_to([128, 16, 16])
                        ib = Ixf[:, hp, :].unsqueeze(1).broadcast_to([128, 16, 16])
                        dve(lambda e, rb=rb: e.tensor_tensor(oh[:, :, :], io16, rb, ALU.is_equal), [iota, rf], [oh])
                        dve(lambda e, ib=ib: e.tensor_tensor(oh[:, :, :], oh[:, :, :], ib, ALU.mult), [oh, Ixf], [oh])
                        dve(lambda e, h=h, dst=dst: e.reduce_sum(dst[:, h * 16:(h + 1) * 16], oh[:, :, :], mybir.AxisListType.X), [oh], [dst])
                mx = top[:, :, 0:1].broadcast_to([128, 8, 16])
                gse = gsel[tl]
                gs3 = gse[:, :].rearrange("p (a b) -> p a b", a=8)
                dve(lambda e, gs3=gs3, mx=mx: e.tensor_tensor(gs3, top[:, :, :], mx, ALU.subtract), [top], [gse])
                fw.op("act", lambda e, gse=gse: e.activation(gse[:, :], gse[:, :], AF.Exp), [gse], [gse])
                dve(lambda e, gs3=gs3: e.reduce_sum(zsum[:, :], gs3, mybir.AxisListType.X), [gse], [zsum])
                dve(lambda e: e.reciprocal(zsum[:, :], zsum[:, :]), [zsum], [zsum])
                dve(lambda e, gs3=gs3: e.tensor_tensor(gs3, gs3, zsum[:, :].unsqueeze(2).broadcast_to([128, 8, 16]), ALU.mult), [gse, zsum], [gse])

        def topk_finish():
            for tl in range(2):
                pb = bank()
                for n_, src in enumerate((Isel[tl], Jsel[tl], gsel[tl])):
                    transpose_f(pb, pb[:, n_ * 128:(n_ + 1) * 128], src, src[:, :])
                evac(ITt, ITt[:, tl, :, :].rearrange("p a b -> p (a b)"), pb, pb[:, 0:384])

        def build_G(g):
            for tl in range(2):
                for qq in range(128 // QT):
                    c0 = qq * QT
                    AT = ATs[qq % 2]
                    BT = BTs[qq % 2]
                    iob = iota[:, :].unsqueeze(1).broadcast_to([128, QT, 128])
                    Ib = ITt[:, tl, 0, c0:c0 + QT].unsqueeze(2).broadcast_to([128, QT, 128])
                    Jb = ITt[:, tl, 1, c0:c0 + QT].unsqueeze(2).broadcast_to([128, QT, 128])
                    gb = ITt[:, tl, 2, c0:c0 + QT].unsqueeze(2).broadcast_to([128, QT, 128])
                    dve(lambda e, Ib=Ib, iob=iob, AT=AT: e.tensor_tensor(AT[:, :, :], iob, Ib, ALU.is_equal), [iota, ITt], [AT])
                    dve(lambda e, Jb=Jb, iob=iob, BT=BT: e.tensor_tensor(BT[:, :, :], iob, Jb, ALU.is_equal), [iota, ITt], [BT])
                    fw.op("pool", lambda e, gb=gb, BT=BT: e.tensor_tensor(BT[:, :, :], BT[:, :, :], gb, ALU.mult), [BT, ITt], [BT])
                    for c4 in range(QT // 4):
                        pb = bank()
                        for k in range(4):
                            c = c4 * 4 + k
                            mm(pb, pb[:, k * 128:(k + 1) * 128], AT, AT[:, c, :], BT, BT[:, c, :], True, True)
                        cg = tl * 128 + c0 + c4 * 4
                        evac(Gall, Gall[:, :, cg:cg + 4], pb, pb[:, :].rearrange("p (c j) -> p j c", c=4), eng="act")

        deferredB = []

        def flushB():
            for a_ in deferredB:
                fw.dma("sp", *a_)
            deferredB.clear()

        bank_set[0] = [0, 1, 2, 3]
        ybanks = [[banks[4], banks[5]], [banks[6], banks[7]]]
        topk_group(0, 0)
        topk_group(0, 1)
        for g in range(NG):
            topk_finish()
            build_G(g)
            for j in range(128):
                w = uring[ur_i[0] % NS]
                ur_i[0] += 1
                fw.dma("sp", w, w[:, :], s_uT, s_uT.t.ap()[j])
                pb = bank()
                for dk in range(16):
                    mm(pb, pb[:, 0:256], w, w[:, dk * 128:(dk + 1) * 128], hTg, hTg[:, :, dk, :], dk == 0, dk == 15)
                gj = ga[j % 2]
                fw.op("act", lambda e, pb=pb, gj=gj: e.activation(gj[:, :], pb[:, 0:256], AF.Gelu), [pb], [gj])
                eng = "dve" if j % 2 == 0 else "pool"
                fw.op(eng, lambda e, gj=gj, j=j: e.tensor_tensor(Gall[:, j, :], Gall[:, j, :], gj[:, :], ALU.mult), [Gall, gj], [Gall])
            flushB()
            hbs = []
            for tl in range(2):
                hb = hbuf[tl]
                fw.dma("sp", hb, hb[:, :], s_h, s_h.t.ap()[2 * g + tl])
                hbs.append(hb)
            for sw in range(2):
                for j in range(128):
                    w = vring[vr_i[0] % NV]
                    vr_i[0] += 1
                    fw.dma("sp", w, w[:, :], s_v, s_v.t.ap()[j][:, sw * 1024:(sw + 1) * 1024])
                    for tl in range(2):
                        for cb in range(2):
                            yb_ = ybanks[tl][cb]
                            mm(yb_, yb_[:, :], Gall, Gall[:, j, tl * 128:(tl + 1) * 128],
                               w, w[:, cb * 512:(cb + 1) * 512], j == 0, j == 127)
                    if j == 7 and g + 1 < NG:
                        topk_group(g + 1, sw)
                for tl in range(2):
                    hb = hbs[tl]
                    for cb in range(2):
                        col = sw * 1024 + cb * 512
                        dve(lambda e, hb=hb, col=col, pbk=ybanks[tl][cb]: e.scalar_tensor_tensor(
                            hb[:, col:col + 512], hb[:, col:col + 512], ALPHA, pbk[:, :],
                            ALU.mult, ALU.add), [ybanks[tl][cb], hb], [hb])
            for tl in range(2):
                gt = 2 * g + tl
                hb = hbs[tl]
                for c in range(4):
                    dve(lambda e, c=c, hb=hb: e.bn_stats(stats[:, c, :], hb[:, c * 512:(c + 1) * 512]), [hb], [stats])
                dve(lambda e: e.bn_aggr(mvar[:, :], stats[:, :, :].rearrange("p a b -> p (a b)")), [stats], [mvar])
                fw.op("act", lambda e: e.activation(rstd[:, :], mvar[:, 1:2], AF.Sqrt, bias=LN_EPS), [mvar], [rstd])
                dve(lambda e: e.reciprocal(rstd[:, :], rstd[:, :]), [rstd], [rstd])
                dve(lambda e, hb=hb: e.tensor_scalar(hb[:, :], hb[:, :], mvar[:, 0:1], rstd[:, 0:1], ALU.subtract, ALU.mult),
                    [hb, mvar, rstd], [hb])
                fw.op("pool", lambda e, hb=hb: e.tensor_tensor(hb[:, :], hb[:, :], g2b[:, :], ALU.mult), [hb, g2b], [hb])
                fw.op("pool", lambda e, hb=hb: e.tensor_tensor(hb[:, :], hb[:, :], b2b[:, :], ALU.add), [hb, b2b], [hb])
                deferredB.append((yout, yout.t.ap()[gt * 128:(gt + 1) * 128, :], hb, hb[:, :]))
        flushB()


def _consts():
    ident = np.eye(128, dtype=np.float32)
    s = np.arange(128)[:, None, None]
    b = np.arange(3)[None, :, None]
    q = np.arange(128)[None, None, :]
    d = np.abs(q - (s + (b - 1) * 128)).astype(np.float32)
    d = np.where(d <= 128, d, NEG_BIG).astype(np.float32)
    iota = np.broadcast_to(np.arange(128, dtype=np.float32)[None, :], (128, 128)).copy()
    return ident, d.reshape(128, 384).copy(), iota


def _segments(x_prompt, x_sample, mem_prompt, mem_sample, seg_len):
    segs = []
    for kind, x, m in (("p", x_prompt, mem_prompt), ("s", x_sample, mem_sample)):
        B, S_, _ = x.shape
        for b in range(B):
            for s0 in range(0, S_, seg_len):
                segs.append((kind, b, s0, S_))
    return segs


def make_in_maps(inputs, n_cores, n_seg, seg_len):
    xp, xs_ = inputs["x_prompt"], inputs["x_sample"]
    mp, ms = inputs["mem_prompt"], inputs["mem_sample"]
    segs = _segments(xp, xs_, mp, ms, seg_len)
    assert len(segs) == n_cores * n_seg
    ident, dist, iota = _consts()
    shared = {
        "c_ident": ident, "c_dist": dist, "c_iota": iota,
        "w_in": np.ascontiguousarray(inputs["w_in"][0]),
        "sink": np.ascontiguousarray(inputs["sink"][0]).reshape(1, 8),
        "conv_w": np.ascontiguousarray(inputs["conv_w"][0].reshape(3, 8, 128).transpose(2, 0, 1).reshape(128, 24)),
        "w_mem_k": np.ascontiguousarray(inputs["w_mem_k"][0]),
        "w_mem_v": np.ascontiguousarray(inputs["w_mem_v"][0]),
        "w_attn_o": np.ascontiguousarray(inputs["w_attn_o"][0]),
        "w_conv_out": np.ascontiguousarray(inputs["w_conv_out"][0]),
        "w_mem_o": np.ascontiguousarray(inputs["w_mem_o"][0]),
        "w_out": np.ascontiguousarray(inputs["w_out"][0]),
        "ln1_g": np.ascontiguousarray(inputs["ln1_g"][0]).reshape(1, D),
        "ln1_b": np.ascontiguousarray(inputs["ln1_b"][0]).reshape(1, D),
        "w_peer_q": np.ascontiguousarray(inputs["w_peer_q"][0]),
        "peer_keys": np.ascontiguousarray(inputs["peer_keys"][0]).reshape(16, 128, 128),
        "peer_u": np.ascontiguousarray(inputs["peer_u"][0]),
        "peer_v": np.ascontiguousarray(inputs["peer_v"][0]),
        "ln2_g": np.ascontiguousarray(inputs["ln2_g"][0]).reshape(1, D),
        "ln2_b": np.ascontiguousarray(inputs["ln2_b"][0]).reshape(1, D),
    }
    in_maps = []
    for c in range(n_cores):
        xcore = np.zeros((n_seg, seg_len + 256, D), np.float32)
        mcore = np.zeros((n_seg, 256, D), np.float32)
        hv = np.zeros((128, n_seg * 2 * 128), np.float32)
        for si in range(n_seg):
            kind, b, s0, S_ = segs[c * n_seg + si]
            x = xp if kind == "p" else xs_
            m = mp if kind == "p" else ms
            lo, hi = s0 - 128, s0 + seg_len + 128
            clo, chi = max(lo, 0), min(hi, S_)
            xcore[si, clo - lo:chi - lo] = x[b, clo:chi]
            mcore[si] = m[b]
            if lo >= 0:
                hv[:, (si * 2 + 0) * 128:(si * 2 + 1) * 128] = 1.0
            if hi <= S_:
                hv[:, (si * 2 + 1) * 128:(si * 2 + 2) * 128] = 1.0
        d = dict(shared)
        d["xc"] = xcore
        d["memc"] = mcore
        d["hval"] = hv
        in_maps.append(d)
    return in_maps, segs


def run(inputs, n_cores=NCORES, n_seg=3, seg_len=SEG, debug_h=False):
    nc = build_program(n_seg=n_seg, seg_tiles=seg_len // 128, debug_h=debug_h)
    in_maps, segs = make_in_maps(inputs, n_cores, n_seg, seg_len)
    res = run_bass_kernel_spmd(nc, in_maps, core_ids=list(range(n_cores)))
    yp = np.zeros(inputs["x_prompt"].shape, np.float32)
    ys = np.zeros(inputs["x_sample"].shape, np.float32)
    for c in range(n_cores):
        y = res.results[c]["yout"]
        for si in range(n_seg):
            kind, b, s0, S_ = segs[c * n_seg + si]
            dst = yp if kind == "p" else ys
            dst[b, s0:s0 + seg_len] = y[si * seg_len:(si + 1) * seg_len]
    return yp, ys


def kernel(**inputs):
    inputs = {k: np.asarray(v) for k, v in inputs.items()}
    return run(inputs)
```

```python
import math
from contextlib import ExitStack

import numpy as np
import concourse.bass as bass
import concourse.mybir as mybir
from concourse.bass_utils import run_bass_kernel_spmd

F32 = mybir.dt.float32
BF16 = mybir.dt.bfloat16
I32 = mybir.dt.int32
U32 = mybir.dt.uint32
AF = mybir.ActivationFunctionType
ALU = mybir.AluOpType

D = 2048
NCORES = 8
SEG = 2048
IN_WIDTH = 11776
ALPHA = 2.0 ** 0.25
LN_EPS = 1e-5
NEG_BIG = 1.0e6


class Buf:
    def __init__(self, name, t, space):
        self.name = name
        self.t = t
        self.space = space
        self.last_w = None
        self.readers = []
        self.dsem = None
        self.dcnt = 0

    def __getitem__(self, k):
        return self.t[k]


class FW:
    ENGS = ("pe", "act", "dve", "pool", "sp")

    def __init__(self, nc):
        self.nc = nc
        self.st = ExitStack()
        self.ops = {e: [] for e in self.ENGS}
        self.cnt = {e: 0 for e in self.ENGS}
        self.cur_sem = {e: None for e in self.ENGS}
        self.seen = {e: {} for e in self.ENGS}
        self.nsem = 0
        self.dma_bufs = []
        self.uid = 0

    def _sem(self, name):
        self.nsem += 1
        return self.st.enter_context(self.nc.semaphore(name))

    def sbuf(self, name, shape, dt, st=None):
        self.uid += 1
        t = (st or self.st).enter_context(self.nc.sbuf_tensor(f"{name}_{self.uid}", list(shape), dt))
        return Buf(name, t, "sb")

    def psum(self, name, shape, dt=F32):
        t = self.st.enter_context(self.nc.psum_tensor(name, list(shape), dt))
        return Buf(name, t, "ps")

    def dram(self, name, shape, dt, kind="Internal"):
        t = self.nc.dram_tensor(name, list(shape), dt, kind=kind)
        return Buf(name, t, "dr")

    def _collect(self, eng, reads, writes):
        waits = {}

        def add(tok):
            if tok is None:
                return
            s, v = tok
            if eng == "pe" and s is self.cur_sem["pe"]:
                return
            if waits.get(s, 0) < v:
                waits[s] = v

        for b in reads:
            add(b.last_w)
        for b in writes:
            add(b.last_w)
            for r in b.readers:
                add(r)
        out = []
        seen = self.seen[eng]
        for s, v in waits.items():
            if seen.get(s, 0) >= v:
                continue
            seen[s] = v
            out.append((s, v))
        return out

    def _commit(self, tok, reads, writes):
        for b in writes:
            b.last_w = tok
            b.readers = []
        for b in reads:
            if b not in writes:
                b.readers.append(tok)
                if len(b.readers) > 64:
                    best = {}
                    for s, v in b.readers:
                        if best.get(s, 0) < v:
                            best[s] = v
                    b.readers = list(best.items())

    def op(self, eng, fn, reads=(), writes=(), inc=True):
        reads = [b for b in reads if b is not None]
        writes = [b for b in writes if b is not None]
        if self.cur_sem[eng] is None:
            self.cur_sem[eng] = self._sem(f"s_{eng}")
        waits = self._collect(eng, reads, writes)
        if inc:
            self.cnt[eng] += 1
            tok = (self.cur_sem[eng], self.cnt[eng])
            self.ops[eng].append((waits, fn, tok[0], 1))
        else:
            tok = (self.cur_sem[eng], self.cnt[eng] + 1)
            self.ops[eng].append((waits, fn, None, 0))
        self._commit(tok, reads, writes)
        return tok

    def dma(self, eng, out_b, out_ap, in_b, in_ap):
        owner = out_b if out_b.space == "sb" else (in_b if in_b.space == "sb" else out_b)
        if owner.dsem is None:
            owner.dsem = self._sem(f"d_{owner.name}")
            self.dma_bufs.append(owner)
        waits = self._collect(eng, [in_b], [out_b])
        owner.dcnt += 16
        tok = (owner.dsem, owner.dcnt)

        def fn(e, out_ap=out_ap, in_ap=in_ap):
            return e.dma_start(out=out_ap, in_=in_ap)

        self.ops[eng].append((waits, fn, tok[0], 16))
        self._commit(tok, [in_b], [out_b])
        return tok

    def barrier(self):
        toks = [(b.dsem, b.dcnt) for b in self.dma_bufs]
        for e in self.ENGS:
            if self.cur_sem[e] is not None:
                toks.append((self.cur_sem[e], self.cnt[e]))
        for e in self.ENGS:
            seen = self.seen[e]
            w = []
            for s, v in toks:
                if e == "pe" and s is self.cur_sem["pe"]:
                    continue
                if seen.get(s, 0) >= v:
                    continue
                seen[s] = v
                w.append((s, v))
            if w:
                self.ops[e].append((w, None, None, 0))

    def final_tokens(self):
        return [(b.dsem, b.dcnt) for b in self.dma_bufs]

    def emit(self):
        nc = self.nc
        ops = self.ops
        final = self.final_tokens()
        for e in self.ENGS:
            if self.cur_sem[e] is not None and e != "sp":
                final.append((self.cur_sem[e], self.cnt[e]))
        with nc.Block() as block:
            def run(e, lst, fin=None):
                for waits, fn, sem, inc in lst:
                    for s, v in waits:
                        e.wait_ge(s, v)
                    if fn is not None:
                        ins = fn(e)
                        if inc:
                            ins.then_inc(sem, inc)
                if fin:
                    for s, v in fin:
                        e.wait_ge(s, v)

            @block.tensor
            def _(e):
                run(e, ops["pe"])

            @block.scalar
            def _(e):
                run(e, ops["act"])

            @block.vector
            def _(e):
                run(e, ops["dve"])

            @block.gpsimd
            def _(e):
                run(e, ops["pool"])

            @block.sync
            def _(e):
                run(e, ops["sp"], final)

    def close(self):
        self.st.close()


def build_program(n_seg=3, seg_tiles=16, debug_h=False):
    nc = bass.Bass("TRN2", target_bir_lowering=False)
    fw = FW(nc)
    NT = n_seg * seg_tiles
    HT = seg_tiles + 2
    assert NT % 2 == 0

    def ein(name, shape, dt=F32):
        return fw.dram(name, shape, dt, kind="ExternalInput")

    xc = ein("xc", [n_seg, HT * 128, D])
    memc = ein("memc", [n_seg, 256, D])
    hval = ein("hval", [128, n_seg * 2 * 128])
    c_ident = ein("c_ident", [128, 128])
    c_dist = ein("c_dist", [128, 3 * 128])
    c_iota = ein("c_iota", [128, 128])
    w_in = ein("w_in", [D, IN_WIDTH])
    sink = ein("sink", [1, 8])
    conv_w = ein("conv_w", [128, 24])
    w_mem_k = ein("w_mem_k", [D, 1024])
    w_mem_v = ein("w_mem_v", [D, 1024])
    w_attn_o = ein("w_attn_o", [1024, D])
    w_conv_out = ein("w_conv_out", [1024, D])
    w_mem_o = ein("w_mem_o", [1024, D])
    w_out = ein("w_out", [D, D])
    ln1_g = ein("ln1_g", [1, D])
    ln1_b = ein("ln1_b", [1, D])
    w_peer_q = ein("w_peer_q", [D, D])
    peer_keys = ein("peer_keys", [16, 128, 128])
    peer_u = ein("peer_u", [16384, D])
    peer_v = ein("peer_v", [16384, D])
    ln2_g = ein("ln2_g", [1, D])
    ln2_b = ein("ln2_b", [1, D])
    yout = fw.dram("yout", [NT * 128, D], F32, kind="ExternalOutput")

    s_wp = fw.dram("s_wp", [22, 128, 4096], BF16)
    s_wg = fw.dram("s_wg", [4, 3, 2, 128, 4096], BF16)
    s_wo = fw.dram("s_wo", [4, 3, 128, 4096], BF16)
    s_wout = fw.dram("s_wout", [8, 128, 4096], BF16)
    s_wq = fw.dram("s_wq", [16, 128, 2048], BF16)
    s_wmk = fw.dram("s_wmk", [4, 128, 4096], BF16)
    s_wmv = fw.dram("s_wmv", [4, 128, 4096], BF16)
    s_uT = fw.dram("s_uT", [128, 128, 2048], BF16)
    s_v = fw.dram("s_v", [128, 128, 2048], BF16)
    s_h = fw.dram("s_h", [NT, 128, D], F32)
    s_hT = fw.dram("s_hT", [NT, 128, D], BF16)

    banks = [fw.psum(f"bank{i}", [128, 512], F32) for i in range(8)]
    bank_i = [0]

    bank_set = [list(range(8))]

    def bank():
        bs = bank_set[0]
        b = banks[bs[bank_i[0] % len(bs)]]
        bank_i[0] += 1
        return b

    ev_i = [0]

    def evac(out_b, out_ap, in_b, in_ap, scale=None, eng=None):
        if eng is None:
            eng = "act" if (ev_i[0] % 2 == 0) else "dve"
            ev_i[0] += 1
        if eng == "act":
            if scale is None:
                fw.op("act", lambda e: e.copy(out_ap, in_ap), [in_b], [out_b])
            else:
                fw.op("act", lambda e: e.mul(out_ap, in_ap, float(scale)), [in_b], [out_b])
        else:
            if scale is None:
                fw.op("dve", lambda e: e.tensor_copy(out_ap, in_ap), [in_b], [out_b])
            else:
                fw.op("dve", lambda e: e.tensor_scalar_mul(out_ap, in_ap, float(scale)), [in_b], [out_b])

    def mm(out_b, out_ap, l_b, l_ap, r_b, r_ap, start, stop, sig=False):
        fw.op("pe", lambda e: e.matmul(out_ap, l_ap, r_ap, start=start, stop=stop), [l_b, r_b], [out_b],
              inc=bool(stop or sig))

    cst = fw.st
    identf = fw.sbuf("identf", [128, 128], F32)
    identb = fw.sbuf("identb", [128, 128], BF16)
    onesb = fw.sbuf("onesb", [128, 128], BF16)
    fw.dma("sp", identf, identf[:, :], c_ident, c_ident.t.ap())
    fw.op("dve", lambda e: e.tensor_copy(identb[:, :], identf[:, :]), [identf], [identb])
    fw.op("dve", lambda e: e.memset(onesb[:, :], 1.0), [], [onesb])

    def transpose_bf(out_b, out_ap, in_b, in_ap):
        fw.op("pe", lambda e: e.transpose(out_ap, in_ap, identb[:, :]), [in_b, identb], [out_b])

    def transpose_f(out_b, out_ap, in_b, in_ap):
        fw.op("pe", lambda e: e.transpose(out_ap, in_ap, identf[:, :]), [in_b, identf], [out_b])

    with ExitStack() as ps:
        stg = [fw.sbuf(f"pp_f{i}", [128, 4096], F32, ps) for i in range(3)]
        stb = [fw.sbuf(f"pp_b{i}", [128, 4096], BF16, ps) for i in range(3)]
        ust = [fw.sbuf(f"pp_u{i}", [128, 2048], BF16, ps) for i in range(2)]
        step = [0]
        cast_engs = ["act", "dve", "pool"]

        def convert(src_b, src_ap, a, b_, dst_b, dst_ap):
            n = a * b_
            i = step[0] % 3
            ce = cast_engs[step[0] % 3]
            step[0] += 1
            f, bb = stg[i], stb[i]
            fw.dma("sp", f, f[:, 0:n].rearrange("p (a b) -> p a b", a=a), src_b, src_ap)
            if ce == "act":
                fw.op("act", lambda e: e.copy(bb[:, 0:n], f[:, 0:n]), [f], [bb])
            elif ce == "dve":
                fw.op("dve", lambda e: e.tensor_copy(bb[:, 0:n], f[:, 0:n]), [f], [bb])
            else:
                fw.op("pool", lambda e: e.tensor_copy(bb[:, 0:n], f[:, 0:n]), [f], [bb])
            fw.dma("act", dst_b, dst_ap, bb, bb[:, 0:n])
            return bb

        def wview(wb, kc):
            return wb.t.ap().rearrange("(kc p) n -> p kc n", p=128)

        win_v = wview(w_in, 16)
        for pc in range(22):
            convert(w_in, win_v[:, :, pc * 256:(pc + 1) * 256], 16, 256, s_wp, s_wp.t.ap()[pc])
        for nb in range(4):
            for br in range(3):
                for hf in range(2):
                    c0 = 5632 + br * 2048 + nb * 512 + hf * 256
                    convert(w_in, win_v[:, :, c0:c0 + 256], 16, 256, s_wg, s_wg.t.ap()[nb, br, hf])
        for br, wb in enumerate((w_attn_o, w_conv_out, w_mem_o)):
            v = wview(wb, 8)
            for nb in range(4):
                convert(wb, v[:, :, nb * 512:(nb + 1) * 512], 8, 512, s_wo, s_wo.t.ap()[nb, br])
        v = wview(w_out, 16)
        for pc in range(8):
            convert(w_out, v[:, :, pc * 256:(pc + 1) * 256], 16, 256, s_wout, s_wout.t.ap()[pc])
        v = wview(w_peer_q, 16)
        for hp in range(16):
            convert(w_peer_q, v[:, :, hp * 128:(hp + 1) * 128], 16, 128, s_wq, s_wq.t.ap()[hp])
        v = wview(w_mem_k, 16)
        for pc in range(4):
            convert(w_mem_k, v[:, :, pc * 256:(pc + 1) * 256], 16, 256, s_wmk, s_wmk.t.ap()[pc])
        v = wview(w_mem_v, 16)
        for pc in range(4):
            convert(w_mem_v, v[:, :, pc * 256:(pc + 1) * 256], 16, 256, s_wmv, s_wmv.t.ap()[pc])
        vv = peer_v.t.ap().rearrange("(i j) d -> j i d", j=128)
        uu = peer_u.t.ap().rearrange("(i j) d -> j i d", j=128)
        for j in range(128):
            convert(peer_v, vv[j].rearrange("p (a b) -> p a b", a=1), 1, 2048, s_v, s_v.t.ap()[j])
        for j in range(128):
            n = 2048
            i = step[0] % 3
            ce = cast_engs[step[0] % 3]
            step[0] += 1
            f, bb = stg[i], stb[i]
            fw.dma("sp", f, f[:, 0:n], peer_u, uu[j])
            if ce == "act":
                fw.op("act", lambda e, bb=bb, f=f: e.copy(bb[:, 0:n], f[:, 0:n]), [f], [bb])
            elif ce == "dve":
                fw.op("dve", lambda e, bb=bb, f=f: e.tensor_copy(bb[:, 0:n], f[:, 0:n]), [f], [bb])
            else:
                fw.op("pool", lambda e, bb=bb, f=f: e.tensor_copy(bb[:, 0:n], f[:, 0:n]), [f], [bb])
            us = ust[j % 2]
            for half in range(2):
                pb = bank()
                pv = pb.t[:, :].bitcast(BF16)
                for k in range(8):
                    dk = half * 8 + k
                    transpose_bf(pb, pv[:, k * 128:(k + 1) * 128], bb, bb[:, dk * 128:(dk + 1) * 128])
                evac(us, us[:, half * 1024:(half + 1) * 1024], pb, pv[:, :], eng="act" if half == 0 else "dve")
            fw.dma("act", s_uT, s_uT.t.ap()[j], us, us[:, :])
    fw.barrier()

    with ExitStack() as pa:
        def S(name, shape, dt):
            return fw.sbuf(name, shape, dt, pa)

        dist = S("dist", [128, 3, 128], F32)
        fw.dma("sp", dist, dist[:, :, :].rearrange("p a b -> p (a b)"), c_dist, c_dist.t.ap())
        hvb = S("hvb", [128, n_seg * 2, 128], BF16)
        esk = S("esk", [128, 8], F32)
        fw.dma("sp", esk, esk[:, :], sink, sink.t.ap().broadcast_to([128, 8]))
        fw.op("act", lambda e: e.activation(esk[:, :], esk[:, :], AF.Exp), [esk], [esk])
        cwt = S("cwt", [128, 3, 8], F32)
        fw.dma("sp", cwt, cwt[:, :, :].rearrange("p a b -> p (a b)"), conv_w, conv_w.t.ap())
        R = 4
        xT = S("xT", [128, R, 16, 128], BF16)
        kT = S("kT", [128, R, 2, 128], BF16)
        vr = S("vr", [128, R, 256], BF16)
        zT = S("zT", [128, R, 8, 128], BF16)
        Q3 = 3
        qT = [S(f"qT{i}", [128, 8, 128], BF16) for i in range(Q3)]
        mqT = [S(f"mqT{i}", [128, 8, 128], BF16) for i in range(Q3)]
        cbT = [S(f"cbT{i}", [128, 8, 128], BF16) for i in range(Q3)]
        xs = [S(f"xs{i}", [128, D], F32) for i in range(2)]
        NHV = n_seg * 2 * 128
        fw.dma("sp", xs[0], xs[0][:, 0:NHV], hval, hval.t.ap())
        fw.op("dve", lambda e: e.tensor_copy(hvb[:, :, :].rearrange("p a b -> p (a b)"), xs[0][:, 0:NHV]), [xs[0]], [hvb])
        qtm = [S(f"qtm{i}", [128, 1024], BF16) for i in range(2)]
        ktm = [S(f"ktm{i}", [128, 256], BF16) for i in range(2)]
        chs = [S(f"chs{i}", [128, 1024], F32) for i in range(2)]
        ztm = [S(f"ztm{i}", [128, 1024], BF16) for i in range(2)]
        cbtm = [S(f"cbtm{i}", [128, 1024], BF16) for i in range(2)]
        mqtm = [S(f"mqtm{i}", [128, 1024], BF16) for i in range(2)]
        tmpS = [S(f"tmpS{i}", [128, 512], F32) for i in range(2)]
        PT = [S(f"PT{i}", [128, 3, 512], BF16) for i in range(2)]
        rden = S("rden", [128, 512], F32)
        attnT = [S(f"attnT{i}", [128, 8, 128], BF16) for i in range(2)]
        convT = [S(f"convT{i}", [128, 8, 128], BF16) for i in range(2)]
        memoT = [S(f"memoT{i}", [128, 8, 128], BF16) for i in range(2)]
        cacc = S("cacc", [128, 8, 128], F32)
        ctmp = S("ctmp", [128, 8, 128], F32)
        PTm = [S(f"PTm{i}", [128, 2, 128], BF16) for i in range(2)]
        rdm = S("rdm", [128, 128], F32)
        mkT = S("mkT", [128, 8, 256], BF16)
        mv = S("mv", [128, 2, 1024], BF16)
        gsb = [S(f"gsb{i}", [128, 3, 512], BF16) for i in range(2)]
        mtmp = [S(f"mtmp{i}", [128, 512], F32) for i in range(2)]
        mgtm = [S(f"mgtm{i}", [128, D], BF16) for i in range(2)]
        mgT = [S(f"mgT{i}", [128, 16, 128], BF16) for i in range(2)]
        hbuf = [S(f"hbuf{i}", [128, D], F32) for i in range(2)]
        memT = hbuf[0]
        memT_v = hbuf[0].t[:, :].bitcast(BF16).rearrange("p (a b) -> p a b", a=16)
        stats = S("stats", [128, 4, 6], F32)
        mvar = S("mvar", [128, 2], F32)
        rstd = S("rstd", [128, 1], F32)
        NW = 3
        wring = [S(f"wr{i}", [128, 4096], BF16) for i in range(NW)]
        wr_i = [0]

        def wload(src_b, src_ap):
            w = wring[wr_i[0] % NW]
            wr_i[0] += 1
            fw.dma("sp", w, w[:, :], src_b, src_ap)
            return w

        slopes = [2.0 ** (-(h + 1)) for h in range(8)]

        def load_xT(seg, ht):
            x_s = xs[ht % 2]
            fw.dma("sp", x_s, x_s[:, :], xc, xc.t.ap()[seg, ht * 128:(ht + 1) * 128, :])
            sl = ht % R
            for q4 in range(4):
                pb = bank()
                for k in range(4):
                    dk = q4 * 4 + k
                    transpose_f(pb, pb[:, k * 128:(k + 1) * 128], x_s, x_s[:, dk * 128:(dk + 1) * 128])
                evac(xT, xT[:, sl, q4 * 4:(q4 + 1) * 4, :].rearrange("p a b -> p (a b)"), pb, pb[:, :])

        def proj_block(hts, pc):
            w = wload(s_wp, s_wp.t.ap()[pc])
            out = []
            for ht in hts:
                sl = ht % R
                pb = bank()
                for dk in range(16):
                    mm(pb, pb[:, 0:256], xT, xT[:, sl, dk, :], w, w[:, dk * 256:(dk + 1) * 256], dk == 0, dk == 15)
                out.append(pb)
            return out

        def tr_group(dst_b, dst_ap, src_b, src, n):
            pb = bank()
            pv = pb.t[:, :].bitcast(BF16)
            for k in range(n):
                transpose_bf(pb, pv[:, k * 128:(k + 1) * 128], src_b, src[:, k * 128:(k + 1) * 128])
            evac(dst_b, dst_ap, pb, pv[:, 0:n * 128])

        def stage_P(seg, hts, fulls):
            fl = [ht for ht, f in zip(hts, fulls) if f]
            idx = {ht: i for i, ht in enumerate(hts)}
            if fl:
                for pc in range(4):
                    for ht, pb in zip(fl, proj_block(fl, pc)):
                        evac(qtm[idx[ht]], qtm[idx[ht]][:, pc * 256:(pc + 1) * 256], pb, pb[:, 0:256], scale=1.0 / math.sqrt(128.0))
            for ht, pb in zip(hts, proj_block(hts, 4)):
                evac(ktm[idx[ht]], ktm[idx[ht]][:, :], pb, pb[:, 0:256])
            for ht, pb in zip(hts, proj_block(hts, 5)):
                evac(vr, vr[:, ht % R, :], pb, pb[:, 0:256])
            for pc in range(4):
                for ht, pb in zip(hts, proj_block(hts, 6 + pc)):
                    evac(chs[idx[ht]], chs[idx[ht]][:, pc * 256:(pc + 1) * 256], pb, pb[:, 0:256])
            if fl:
                for pc in range(4):
                    for ht, pb in zip(fl, proj_block(fl, 10 + pc)):
                        evac(cbtm[idx[ht]], cbtm[idx[ht]][:, pc * 256:(pc + 1) * 256], pb, pb[:, 0:256])
            for pc in range(4):
                for ht, pb in zip(hts, proj_block(hts, 14 + pc)):
                    i = idx[ht]
                    fw.op("dve", lambda e, pb=pb, pc=pc, i=i: e.tensor_tensor(
                        ztm[i][:, pc * 256:(pc + 1) * 256], pb[:, 0:256], chs[i][:, pc * 256:(pc + 1) * 256], ALU.mult),
                        [pb, chs[i]], [ztm[i]])
            if fl:
                for pc in range(4):
                    for ht, pb in zip(fl, proj_block(fl, 18 + pc)):
                        evac(mqtm[idx[ht]], mqtm[idx[ht]][:, pc * 256:(pc + 1) * 256], pb, pb[:, 0:256], scale=1.0 / 16.0)
            for ht, f in zip(hts, fulls):
                i = idx[ht]
                sl = ht % R
                tr_group(kT, kT[:, sl, :, :].rearrange("p a b -> p (a b)"), ktm[i], ktm[i], 2)
                tr_group(zT, zT[:, sl, :, :].rearrange("p a b -> p (a b)"), ztm[i], ztm[i], 8)
                if f:
                    q3 = ht % Q3
                    tr_group(qT[q3], qT[q3][:, :, :].rearrange("p a b -> p (a b)"), qtm[i], qtm[i], 8)
                    tr_group(cbT[q3], cbT[q3][:, :, :].rearrange("p a b -> p (a b)"), cbtm[i], cbtm[i], 8)
                    tr_group(mqT[q3], mqT[q3][:, :, :].rearrange("p a b -> p (a b)"), mqtm[i], mqtm[i], 8)

        def seg_memory(seg):
            for mt in range(2):
                x_s = xs[mt]
                fw.dma("sp", x_s, x_s[:, :], memc, memc.t.ap()[seg, mt * 128:(mt + 1) * 128, :])
                for q4 in range(4):
                    pb = bank()
                    for k in range(4):
                        dk = q4 * 4 + k
                        transpose_f(pb, pb[:, k * 128:(k + 1) * 128], x_s, x_s[:, dk * 128:(dk + 1) * 128])
                    evac(memT, memT_v[:, q4 * 4:(q4 + 1) * 4, mt * 128:(mt + 1) * 128],
                         pb, pb[:, :].rearrange("p (a b) -> p a b", a=4))
            for pc in range(4):
                w = wload(s_wmk, s_wmk.t.ap()[pc])
                for c2 in range(2):
                    pb = bank()
                    for dk in range(16):
                        mm(pb, pb[:, 0:256], w, w[:, dk * 256 + c2 * 128: dk * 256 + c2 * 128 + 128],
                           memT, memT_v[:, dk, :], dk == 0, dk == 15)
                    evac(mkT, mkT[:, pc * 2 + c2, :], pb, pb[:, 0:256])
            for pc in range(4):
                w = wload(s_wmv, s_wmv.t.ap()[pc])
                for mt in range(2):
                    pb = bank()
                    for dk in range(16):
                        mm(pb, pb[:, 0:256], memT, memT_v[:, dk, mt * 128:(mt + 1) * 128],
                           w, w[:, dk * 256:(dk + 1) * 256], dk == 0, dk == 15)
                    evac(mv, mv[:, mt, pc * 256:(pc + 1) * 256], pb, pb[:, 0:256])

        def mixers(seg, ht, ti):
            sl = ht % R
            q3 = ht % Q3
            q_t, mq_t, cb_t = qT[q3], mqT[q3], cbT[q3]
            aT, cT, mT = attnT[ti], convT[ti], memoT[ti]
            for kvh in range(2):
                pt = PT[kvh]
                for bi, bo in enumerate((-1, 0, 1)):
                    ks = (ht + bo) % R
                    pb = bank()
                    mm(pb, pb[:, :], kT, kT[:, ks, kvh, :], q_t, q_t[:, kvh * 4:(kvh + 1) * 4, :], True, True)
                    tm = tmpS[bi % 2]
                    for hh in range(4):
                        h = kvh * 4 + hh
                        fw.op("dve", lambda e, pb=pb, tm=tm, hh=hh, h=h, bi=bi: e.scalar_tensor_tensor(
                            tm[:, hh * 128:(hh + 1) * 128], dist[:, bi, :], -slopes[h], pb[:, hh * 128:(hh + 1) * 128],
                            ALU.mult, ALU.add), [pb, dist], [tm])
                    fw.op("act", lambda e, tm=tm, pt=pt, bi=bi: e.activation(pt[:, bi, :], tm[:, :], AF.Exp), [tm], [pt])
                po = bank()
                pd = bank()
                for bi, bo in enumerate((-1, 0, 1)):
                    ks = (ht + bo) % R
                    mm(po, po[:, :], vr, vr[:, ks, kvh * 128:(kvh + 1) * 128], pt, pt[:, bi, :], bi == 0, bi == 2)
                for bi, bo in enumerate((-1, 0, 1)):
                    t_abs = ht + bo
                    if t_abs == 0:
                        vb, va = hvb, hvb[:, seg * 2 + 0, :]
                    elif t_abs == HT - 1:
                        vb, va = hvb, hvb[:, seg * 2 + 1, :]
                    else:
                        vb, va = onesb, onesb[:, :]
                    mm(pd, pd[:, :], vb, va, pt, pt[:, bi, :], bi == 0, bi == 2)
                for hh in range(4):
                    h = kvh * 4 + hh
                    fw.op("dve", lambda e, pd=pd, hh=hh, h=h: e.tensor_scalar(
                        rden[:, hh * 128:(hh + 1) * 128], pd[:, hh * 128:(hh + 1) * 128], esk[:, h:h + 1], None,
                        ALU.add), [pd, esk], [rden])
                fw.op("dve", lambda e: e.reciprocal(rden[:, :], rden[:, :]), [rden], [rden])
                fw.op("dve", lambda e, po=po, kvh=kvh, aT=aT: e.tensor_tensor(
                    aT[:, kvh * 4:(kvh + 1) * 4, :].rearrange("p a b -> p (a b)"), po[:, :], rden[:, :], ALU.mult),
                    [po, rden], [aT])
            sp_, sn_ = (ht - 1) % R, (ht + 1) % R

            def cw(k, n):
                return cwt[:, k, :].unsqueeze(2).broadcast_to([128, 8, n])

            P = "pool"
            fw.op(P, lambda e: e.tensor_tensor(cacc[:, :, :], zT[:, sl, :, :], cw(1, 128), ALU.mult), [zT, cwt], [cacc])
            fw.op(P, lambda e: e.tensor_tensor(ctmp[:, :, 1:128], zT[:, sl, :, 0:127], cw(0, 127), ALU.mult), [zT, cwt], [ctmp])
            fw.op(P, lambda e: e.tensor_tensor(ctmp[:, :, 0:1], zT[:, sp_, :, 127:128], cw(0, 1), ALU.mult), [zT, cwt], [ctmp])
            fw.op(P, lambda e: e.tensor_tensor(cacc[:, :, :], cacc[:, :, :], ctmp[:, :, :], ALU.add), [ctmp, cacc], [cacc])
            fw.op(P, lambda e: e.tensor_tensor(ctmp[:, :, 0:127], zT[:, sl, :, 1:128], cw(2, 127), ALU.mult), [zT, cwt], [ctmp])
            fw.op(P, lambda e: e.tensor_tensor(ctmp[:, :, 127:128], zT[:, sn_, :, 0:1], cw(2, 1), ALU.mult), [zT, cwt], [ctmp])
            fw.op(P, lambda e: e.tensor_tensor(cacc[:, :, :], cacc[:, :, :], ctmp[:, :, :], ALU.add), [ctmp, cacc], [cacc])
            fw.op(P, lambda e: e.tensor_tensor(cT[:, :, :], cacc[:, :, :], cb_t[:, :, :], ALU.mult), [cacc, cb_t], [cT])
            for h in range(4):
                ptm = PTm[h % 2]
                for mc in range(2):
                    pb = bank()
                    for dh in range(2):
                        mm(pb, pb[:, 0:128], mkT, mkT[:, 2 * h + dh, mc * 128:(mc + 1) * 128],
                           mq_t, mq_t[:, 2 * h + dh, :], dh == 0, dh == 1)
                    fw.op("act", lambda e, pb=pb, ptm=ptm, mc=mc: e.activation(ptm[:, mc, :], pb[:, 0:128], AF.Exp), [pb], [ptm])
                pd = bank()
                for mc in range(2):
                    mm(pd, pd[:, 0:128], onesb, onesb[:, :], ptm, ptm[:, mc, :], mc == 0, mc == 1)
                fw.op("dve", lambda e, pd=pd: e.reciprocal(rdm[:, :], pd[:, 0:128]), [pd], [rdm])
                for dh in range(2):
                    po = bank()
                    for mc in range(2):
                        mm(po, po[:, 0:128], mv, mv[:, mc, h * 256 + dh * 128: h * 256 + dh * 128 + 128],
                           ptm, ptm[:, mc, :], mc == 0, mc == 1)
                    fw.op("dve", lambda e, po=po, h=h, dh=dh, mT=mT: e.tensor_tensor(
                        mT[:, 2 * h + dh, :], po[:, 0:128], rdm[:, :], ALU.mult), [po, rdm], [mT])

        def stage_A(seg, hts, gtiles):
            nt = len(hts)
            for ti, ht in enumerate(hts):
                mixers(seg, ht, ti)
            for nb in range(4):
                for br in range(3):
                    pgs = [bank() for _ in range(nt)]
                    for hf in range(2):
                        w = wload(s_wg, s_wg.t.ap()[nb, br, hf])
                        for ti, ht in enumerate(hts):
                            pg = pgs[ti]
                            for dk in range(16):
                                mm(pg, pg[:, hf * 256:(hf + 1) * 256], xT, xT[:, ht % R, dk, :],
                                   w, w[:, dk * 256:(dk + 1) * 256], dk == 0, dk == 15)
                    for ti in range(nt):
                        fw.op("act", lambda e, pg=pgs[ti], g=gsb[ti], br=br: e.activation(g[:, br, :], pg[:, :], AF.Sigmoid), [pgs[ti]], [gsb[ti]])
                pos_ = [[None] * 3 for _ in range(nt)]
                for br in range(3):
                    w = wload(s_wo, s_wo.t.ap()[nb, br])
                    for ti in range(nt):
                        src = (attnT[ti], convT[ti], memoT[ti])[br]
                        po = bank()
                        for kc in range(8):
                            mm(po, po[:, :], src, src[:, kc, :], w, w[:, kc * 512:(kc + 1) * 512], kc == 0, kc == 7)
                        pos_[ti][br] = po
                m0, m1 = mtmp
                for ti in range(nt):
                    g = gsb[ti]
                    p0, p1, p2 = pos_[ti]
                    fw.op("dve", lambda e, po=p0, g=g: e.tensor_tensor(m0[:, :], po[:, :], g[:, 0, :], ALU.mult), [p0, g], [m0])
                    fw.op("dve", lambda e, po=p1, g=g: e.tensor_tensor(m1[:, :], po[:, :], g[:, 1, :], ALU.mult), [p1, g], [m1])
                    fw.op("dve", lambda e: e.tensor_tensor(m0[:, :], m0[:, :], m1[:, :], ALU.add), [m0, m1], [m0])
                    fw.op("dve", lambda e, po=p2, g=g: e.tensor_tensor(m1[:, :], po[:, :], g[:, 2, :], ALU.mult), [p2, g], [m1])
                    fw.op("dve", lambda e, nb=nb, mg=mgtm[ti]: e.tensor_tensor(mg[:, nb * 512:(nb + 1) * 512], m0[:, :], m1[:, :], ALU.add), [m0, m1], [mgtm[ti]])
            for ti in range(nt):
                for half in range(2):
                    tr_group(mgT[ti], mgT[ti][:, half * 8:(half + 1) * 8, :].rearrange("p a b -> p (a b)"),
                             mgtm[ti], mgtm[ti][:, half * 1024:(half + 1) * 1024], 8)
            for ti, ht in enumerate(hts):
                hb = hbuf[ti]
                fw.dma("sp", hb, hb[:, :], xc, xc.t.ap()[seg, ht * 128:(ht + 1) * 128, :])
            for pc in range(8):
                w = wload(s_wout, s_wout.t.ap()[pc])
                for ti in range(nt):
                    hb = hbuf[ti]
                    pb = bank()
                    for dk in range(16):
                        mm(pb, pb[:, 0:256], mgT[ti], mgT[ti][:, dk, :], w, w[:, dk * 256:(dk + 1) * 256], dk == 0, dk == 15)
                    fw.op("dve", lambda e, pb=pb, pc=pc, hb=hb: e.scalar_tensor_tensor(
                        hb[:, pc * 256:(pc + 1) * 256], hb[:, pc * 256:(pc + 1) * 256], ALPHA, pb[:, 0:256],
                        ALU.mult, ALU.add), [pb, hb], [hb])
            g1b = wring[wr_i[0] % NW]
            wr_i[0] += 1
            b1b = wring[wr_i[0] % NW]
            wr_i[0] += 1
            g1v = g1b.t[:, :].bitcast(F32)
            b1v = b1b.t[:, :].bitcast(F32)
            fw.dma("sp", g1b, g1v, ln1_g, ln1_g.t.ap().broadcast_to([128, D]))
            fw.dma("sp", b1b, b1v, ln1_b, ln1_b.t.ap().broadcast_to([128, D]))
            for ti, ht in enumerate(hts):
                hb = hbuf[ti]
                gtile = gtiles[ti]
                layer_norm(hb, g1b, b1b, g1v, b1v)
                hb16 = mgtm[ti]
                hTs = mgT[ti]
                fw.op("act", lambda e, hb=hb, hb16=hb16: e.copy(hb16[:, :], hb[:, :]), [hb], [hb16])
                deferredA.append((s_h, s_h.t.ap()[gtile], hb, hb[:, :]))
                for half in range(2):
                    tr_group(hTs, hTs[:, half * 8:(half + 1) * 8, :].rearrange("p a b -> p (a b)"),
                             hb16, hb16[:, half * 1024:(half + 1) * 1024], 8)
                deferredA.append((s_hT, s_hT.t.ap()[gtile], hTs, hTs[:, :, :].rearrange("p a b -> p (a b)")))

        def layer_norm(hb, gb, bb, gv, bv):
            for c in range(4):
                fw.op("dve", lambda e, c=c: e.bn_stats(stats[:, c, :], hb[:, c * 512:(c + 1) * 512]), [hb], [stats])
            fw.op("dve", lambda e: e.bn_aggr(mvar[:, :], stats[:, :, :].rearrange("p a b -> p (a b)")), [stats], [mvar])
            fw.op("act", lambda e: e.activation(rstd[:, :], mvar[:, 1:2], AF.Sqrt, bias=LN_EPS), [mvar], [rstd])
            fw.op("dve", lambda e: e.reciprocal(rstd[:, :], rstd[:, :]), [rstd], [rstd])
            fw.op("dve", lambda e: e.tensor_scalar(hb[:, :], hb[:, :], mvar[:, 0:1], rstd[:, 0:1], ALU.subtract, ALU.mult),
                  [hb, mvar, rstd], [hb])
            fw.op("pool", lambda e: e.tensor_tensor(hb[:, :], hb[:, :], gv, ALU.mult), [hb, gb], [hb])
            fw.op("pool", lambda e: e.tensor_tensor(hb[:, :], hb[:, :], bv, ALU.add), [hb, bb], [hb])

        deferredA = []

        def flushA():
            for a_ in deferredA:
                fw.dma("sp", *a_)
            deferredA.clear()

        assert seg_tiles % 2 == 0
        for seg in range(n_seg):
            seg_memory(seg)
            load_xT(seg, 0)
            stage_P(seg, [0], [False])
            load_xT(seg, 1)
            stage_P(seg, [1], [True])
            for i in range(seg_tiles // 2):
                a, b = 2 * i + 2, 2 * i + 3
                load_xT(seg, a)
                load_xT(seg, b)
                stage_P(seg, [a, b], [True, b <= seg_tiles])
                flushA()
                stage_A(seg, [2 * i + 1, 2 * i + 2], [seg * seg_tiles + 2 * i, seg * seg_tiles + 2 * i + 1])
            flushA()
    fw.barrier()

    if not debug_h:
        build_phase_B(nc, fw, locals())
    else:
        with ExitStack() as pdbg:
            hb = fw.sbuf("dbg", [128, D], F32, pdbg)
            for gt in range(NT):
                fw.dma("sp", hb, hb[:, :], s_h, s_h.t.ap()[gt])
                fw.dma("sp", yout, yout.t.ap()[gt * 128:(gt + 1) * 128, :], hb, hb[:, :])
    fw.emit()
    fw.close()
    return nc


def build_phase_B(nc, fw, env):
    NT = env["NT"]
    bank = env["bank"]
    bank_set = env["bank_set"]
    banks = env["banks"]
    evac = env["evac"]
    mm = env["mm"]
    transpose_bf = env["transpose_bf"]
    transpose_f = env["transpose_f"]
    identf = env["identf"]
    s_wq, s_uT, s_v, s_h, s_hT, yout = (env[k] for k in ("s_wq", "s_uT", "s_v", "s_h", "s_hT", "yout"))
    peer_keys, ln2_g, ln2_b, c_iota = (env[k] for k in ("peer_keys", "ln2_g", "ln2_b", "c_iota"))
    NG = NT // 2
    with ExitStack() as pb_:
        def S(name, shape, dt):
            return fw.sbuf(name, shape, dt, pb_)

        g2b = S("g2b", [128, D], F32)
        b2b = S("b2b", [128, D], F32)
        fw.dma("sp", g2b, g2b[:, :], ln2_g, ln2_g.t.ap().broadcast_to([128, D]))
        fw.dma("sp", b2b, b2b[:, :], ln2_b, ln2_b.t.ap().broadcast_to([128, D]))
        iota = S("iota", [128, 128], F32)
        fw.dma("sp", iota, iota[:, :], c_iota, c_iota.t.ap())
        keysT = S("keysT", [128, 16, 128], BF16)
        kst = S("kst", [128, 128], F32)
        for hp in range(16):
            fw.dma("sp", kst, kst[:, :], peer_keys, peer_keys.t.ap()[hp])
            pb = bank()
            transpose_f(pb, pb[:, 0:128], kst, kst[:, :])
            evac(keysT, keysT[:, hp, :], pb, pb[:, 0:128])

        hTg = S("hTg", [128, 2, 16, 128], BF16)
        qpT = S("qpT", [128, 16, 256], BF16)
        sc = S("sc", [128, 16, 128], F32)
        T = S("T", [128, 16, 16], F32)
        Ix = S("Ix", [128, 16, 16], U32)
        Ixf = S("Ixf", [128, 16, 16], F32)
        wk = S("wk", [128, 128], F32)
        C = S("C", [128, 16, 16], F32)
        wk2 = S("wk2", [128, 256], F32)
        top = S("top", [128, 8, 16], F32)
        pos = S("pos", [128, 8, 16], U32)
        r1i = S("r1i", [128, 8, 16], I32)
        r2i = S("r2i", [128, 8, 16], I32)
        r1f = S("r1f", [128, 8, 16], F32)
        r2f = S("r2f", [128, 8, 16], F32)
        oh = S("oh", [128, 16, 16], F32)
        Isel = [S(f"Isel{i}", [128, 128], F32) for i in range(2)]
        Jsel = [S(f"Jsel{i}", [128, 128], F32) for i in range(2)]
        gsel = [S(f"gsel{i}", [128, 128], F32) for i in range(2)]
        zsum = S("zsum", [128, 8], F32)
        ITt = S("ITt", [128, 2, 3, 128], F32)
        QT = 16
        ATs = [S(f"AT{i}", [128, QT, 128], BF16) for i in range(2)]
        BTs = [S(f"BT{i}", [128, QT, 128], BF16) for i in range(2)]
        Gall = S("Gall", [128, 128, 256], BF16)
        NS = 4
        uring = [S(f"ur{i}", [128, 2048], BF16) for i in range(NS)]
        NV = 8
        vring = [S(f"vr{i}", [128, 1024], BF16) for i in range(NV)]
        ga = [S(f"ga{i}", [128, 256], BF16) for i in range(2)]
        hbuf = [S(f"hbB{i}", [128, D], F32) for i in range(2)]
        stats = S("statsB", [128, 4, 6], F32)
        mvar = S("mvarB", [128, 2], F32)
        rstd = S("rstdB", [128, 1], F32)
        ur_i = [0]
        vr_i = [0]

        def V(e):
            return e

        def dve(fn, r, w):
            fw.op("dve", fn, r, w)

        def topk_group(g, part):
            if part == 0:
              for tl in range(2):
                fw.dma("sp", hTg, hTg[:, tl, :, :].rearrange("p a b -> p (a b)"), s_hT, s_hT.t.ap()[2 * g + tl])
              for hp in range(16):
                w = uring[ur_i[0] % NS]
                ur_i[0] += 1
                fw.dma("sp", w, w[:, :], s_wq, s_wq.t.ap()[hp])
                pb = bank()
                for dk in range(16):
                    mm(pb, pb[:, 0:256], w, w[:, dk * 128:(dk + 1) * 128], hTg, hTg[:, :, dk, :], dk == 0, dk == 15)
                evac(qpT, qpT[:, hp, :], pb, pb[:, 0:256])
            for tl in (part,):
                for q4 in range(4):
                    pb = bank()
                    for k in range(4):
                        hp = q4 * 4 + k
                        mm(pb, pb[:, k * 128:(k + 1) * 128], qpT, qpT[:, hp, tl * 128:(tl + 1) * 128],
                           keysT, keysT[:, hp, :], True, True)
                    evac(sc, sc[:, q4 * 4:(q4 + 1) * 4, :].rearrange("p a b -> p (a b)"), pb, pb[:, :])
                for hp in range(16):
                    dve(lambda e, hp=hp: e.max(out=T[:, hp, 0:8], in_=sc[:, hp, :]), [sc], [T])
                    dve(lambda e, hp=hp: e.max_index(out=Ix[:, hp, 0:8], in_max=T[:, hp, 0:8], in_values=sc[:, hp, :]), [sc, T], [Ix])
                    dve(lambda e, hp=hp: e.match_replace(out=wk[:, :], in_to_replace=T[:, hp, 0:8], in_values=sc[:, hp, :], imm_value=-1e30), [sc, T], [wk])
                    dve(lambda e, hp=hp: e.max(out=T[:, hp, 8:16], in_=wk[:, :]), [wk], [T])
                    dve(lambda e, hp=hp: e.max_index(out=Ix[:, hp, 8:16], in_max=T[:, hp, 8:16], in_values=wk[:, :]), [wk, T], [Ix])
                dve(lambda e: e.tensor_copy(Ixf[:, :, :], Ix[:, :, :].bitcast(I32)), [Ix], [Ixf])
                for h in range(8):
                    a1 = T[:, 2 * h, :].unsqueeze(2).broadcast_to([128, 16, 16])
                    a2 = T[:, 2 * h + 1, :].unsqueeze(1).broadcast_to([128, 16, 16])
                    Cf = C[:, :, :].rearrange("p a b -> p (a b)")
                    dve(lambda e, a1=a1, a2=a2: e.tensor_tensor(C[:, :, :], a1, a2, ALU.add), [T], [C])
                    dve(lambda e, h=h, Cf=Cf: e.max(out=top[:, h, 0:8], in_=Cf), [C], [top])
                    dve(lambda e, h=h, Cf=Cf: e.max_index(out=pos[:, h, 0:8], in_max=top[:, h, 0:8], in_values=Cf), [C, top], [pos])
                    dve(lambda e, h=h, Cf=Cf: e.match_replace(out=wk2[:, :], in_to_replace=top[:, h, 0:8], in_values=Cf, imm_value=-1e30), [C, top], [wk2])
                    dve(lambda e, h=h: e.max(out=top[:, h, 8:16], in_=wk2[:, :]), [wk2], [top])
                    dve(lambda e, h=h: e.max_index(out=pos[:, h, 8:16], in_max=top[:, h, 8:16], in_values=wk2[:, :]), [wk2, top], [pos])
                posi = pos[:, :, :].bitcast(I32)
                dve(lambda e, posi=posi: e.tensor_single_scalar(out=r1i[:, :, :], in_=posi, scalar=4, op=ALU.arith_shift_right), [pos], [r1i])
                dve(lambda e, posi=posi: e.tensor_single_scalar(out=r2i[:, :, :], in_=posi, scalar=15, op=ALU.bitwise_and), [pos], [r2i])
                dve(lambda e: e.tensor_copy(r1f[:, :, :], r1i[:, :, :]), [r1i], [r1f])
                dve(lambda e: e.tensor_copy(r2f[:, :, :], r2i[:, :, :]), [r2i], [r2f])
                io16 = iota[:, 0:16].unsqueeze(1).broadcast_to([128, 16, 16])
                for h in range(8):
                    for (rf, hp, dst) in ((r1f, 2 * h, Isel[tl]), (r2f, 2 * h + 1, Jsel[tl])):
                        rb = rf[:, h, :].unsqueeze(2).broadcast_to([128, 16, 16])
                        ib = Ixf[:, hp, :].unsqueeze(1).broadcast_to([128, 16, 16])
                        dve(lambda e, rb=rb: e.tensor_tensor(oh[:, :, :], io16, rb, ALU.is_equal), [iota, rf], [oh])
                        dve(lambda e, ib=ib: e.tensor_tensor(oh[:, :, :], oh[:, :, :], ib, ALU.mult), [oh, Ixf], [oh])
                        dve(lambda e, h=h, dst=dst: e.reduce_sum(dst[:, h * 16:(h + 1) * 16], oh[:, :, :], mybir.AxisListType.X), [oh], [dst])
                mx = top[:, :, 0:1].broadcast_to([128, 8, 16])
                gse = gsel[tl]
                gs3 = gse[:, :].rearrange("p (a b) -> p a b", a=8)
                dve(lambda e, gs3=gs3, mx=mx: e.tensor_tensor(gs3, top[:, :, :], mx, ALU.subtract), [top], [gse])
                fw.op("act", lambda e, gse=gse: e.activation(gse[:, :], gse[:, :], AF.Exp), [gse], [gse])
                dve(lambda e, gs3=gs3: e.reduce_sum(zsum[:, :], gs3, mybir.AxisListType.X), [gse], [zsum])
                dve(lambda e: e.reciprocal(zsum[:, :], zsum[:, :]), [zsum], [zsum])
                dve(lambda e, gs3=gs3: e.tensor_tensor(gs3, gs3, zsum[:, :].unsqueeze(2).broadcast_to([128, 8, 16]), ALU.mult), [gse, zsum], [gse])

        def topk_finish():
            for tl in range(2):
                pb = bank()
                for n_, src in enumerate((Isel[tl], Jsel[tl], gsel[tl])):
                    transpose_f(pb, pb[:, n_ * 128:(n_ + 1) * 128], src, src[:, :])
                evac(ITt, ITt[:, tl, :, :].rearrange("p a b -> p (a b)"), pb, pb[:, 0:384])

        def build_G(g):
            for tl in range(2):
                for qq in range(128 // QT):
                    c0 = qq * QT
                    AT = ATs[qq % 2]
                    BT = BTs[qq % 2]
                    iob = iota[:, :].unsqueeze(1).broadcast_to([128, QT, 128])
                    Ib = ITt[:, tl, 0, c0:c0 + QT].unsqueeze(2).broadcast_to([128, QT, 128])
                    Jb = ITt[:, tl, 1, c0:c0 + QT].unsqueeze(2).broadcast_to([128, QT, 128])
                    gb = ITt[:, tl, 2, c0:c0 + QT].unsqueeze(2).broadcast_to([128, QT, 128])
                    dve(lambda e, Ib=Ib, iob=iob, AT=AT: e.tensor_tensor(AT[:, :, :], iob, Ib, ALU.is_equal), [iota, ITt], [AT])
                    dve(lambda e, Jb=Jb, iob=iob, BT=BT: e.tensor_tensor(BT[:, :, :], iob, Jb, ALU.is_equal), [iota, ITt], [BT])
                    fw.op("pool", lambda e, gb=gb, BT=BT: e.tensor_tensor(BT[:, :, :], BT[:, :, :], gb, ALU.mult), [BT, ITt], [BT])
                    for c4 in range(QT // 4):
                        pb = bank()
                        for k in range(4):
                            c = c4 * 4 + k
                            mm(pb, pb[:, k * 128:(k + 1) * 128], AT, AT[:, c, :], BT, BT[:, c, :], True, True)
                        cg = tl * 128 + c0 + c4 * 4
                        evac(Gall, Gall[:, :, cg:cg + 4], pb, pb[:, :].rearrange("p (c j) -> p j c", c=4), eng="act")

        deferredB = []

        def flushB():
            for a_ in deferredB:
                fw.dma("sp", *a_)
            deferredB.clear()

        bank_set[0] = [0, 1, 2, 3]
        ybanks = [[banks[4], banks[5]], [banks[6], banks[7]]]
        topk_group(0, 0)
        topk_group(0, 1)
        for g in range(NG):
            topk_finish()
            build_G(g)
            for j in range(128):
                w = uring[ur_i[0] % NS]
                ur_i[0] += 1
                fw.dma("sp", w, w[:, :], s_uT, s_uT.t.ap()[j])
                pb = bank()
                for dk in range(16):
                    mm(pb, pb[:, 0:256], w, w[:, dk * 128:(dk + 1) * 128], hTg, hTg[:, :, dk, :], dk == 0, dk == 15)
                gj = ga[j % 2]
                fw.op("act", lambda e, pb=pb, gj=gj: e.activation(gj[:, :], pb[:, 0:256], AF.Gelu), [pb], [gj])
                eng = "dve" if j % 2 == 0 else "pool"
                fw.op(eng, lambda e, gj=gj, j=j: e.tensor_tensor(Gall[:, j, :], Gall[:, j, :], gj[:, :], ALU.mult), [Gall, gj], [Gall])
            flushB()
            hbs = []
            for tl in range(2):
                hb = hbuf[tl]
                fw.dma("sp", hb, hb[:, :], s_h, s_h.t.ap()[2 * g + tl])
                hbs.append(hb)
            for sw in range(2):
                for j in range(128):
                    w = vring[vr_i[0] % NV]
                    vr_i[0] += 1
                    fw.dma("sp", w, w[:, :], s_v, s_v.t.ap()[j][:, sw * 1024:(sw + 1) * 1024])
                    for tl in range(2):
                        for cb in range(2):
                            yb_ = ybanks[tl][cb]
                            mm(yb_, yb_[:, :], Gall, Gall[:, j, tl * 128:(tl + 1) * 128],
                               w, w[:, cb * 512:(cb + 1) * 512], j == 0, j == 127, sig=(tl == 1 and cb == 1))
                    if j == 7 and g + 1 < NG:
                        topk_group(g + 1, sw)
                for tl in range(2):
                    hb = hbs[tl]
                    for cb in range(2):
                        col = sw * 1024 + cb * 512
                        dve(lambda e, hb=hb, col=col, pbk=ybanks[tl][cb]: e.scalar_tensor_tensor(
                            hb[:, col:col + 512], hb[:, col:col + 512], ALPHA, pbk[:, :],
                            ALU.mult, ALU.add), [ybanks[tl][cb], hb], [hb])
            for tl in range(2):
                gt = 2 * g + tl
                hb = hbs[tl]
                for c in range(4):
                    dve(lambda e, c=c, hb=hb: e.bn_stats(stats[:, c, :], hb[:, c * 512:(c + 1) * 512]), [hb], [stats])
                dve(lambda e: e.bn_aggr(mvar[:, :], stats[:, :, :].rearrange("p a b -> p (a b)")), [stats], [mvar])
                fw.op("act", lambda e: e.activation(rstd[:, :], mvar[:, 1:2], AF.Sqrt, bias=LN_EPS), [mvar], [rstd])
                dve(lambda e: e.reciprocal(rstd[:, :], rstd[:, :]), [rstd], [rstd])
                dve(lambda e, hb=hb: e.tensor_scalar(hb[:, :], hb[:, :], mvar[:, 0:1], rstd[:, 0:1], ALU.subtract, ALU.mult),
                    [hb, mvar, rstd], [hb])
                fw.op("pool", lambda e, hb=hb: e.tensor_tensor(hb[:, :], hb[:, :], g2b[:, :], ALU.mult), [hb, g2b], [hb])
                fw.op("pool", lambda e, hb=hb: e.tensor_tensor(hb[:, :], hb[:, :], b2b[:, :], ALU.add), [hb, b2b], [hb])
                deferredB.append((yout, yout.t.ap()[gt * 128:(gt + 1) * 128, :], hb, hb[:, :]))
        flushB()


def _consts():
    ident = np.eye(128, dtype=np.float32)
    s = np.arange(128)[:, None, None]
    b = np.arange(3)[None, :, None]
    q = np.arange(128)[None, None, :]
    d = np.abs(q - (s + (b - 1) * 128)).astype(np.float32)
    d = np.where(d <= 128, d, NEG_BIG).astype(np.float32)
    iota = np.broadcast_to(np.arange(128, dtype=np.float32)[None, :], (128, 128)).copy()
    return ident, d.reshape(128, 384).copy(), iota


def _segments(x_prompt, x_sample, mem_prompt, mem_sample, seg_len):
    segs = []
    for kind, x, m in (("p", x_prompt, mem_prompt), ("s", x_sample, mem_sample)):
        B, S_, _ = x.shape
        for b in range(B):
            for s0 in range(0, S_, seg_len):
                segs.append((kind, b, s0, S_))
    return segs


def make_in_maps(inputs, n_cores, n_seg, seg_len):
    xp, xs_ = inputs["x_prompt"], inputs["x_sample"]
    mp, ms = inputs["mem_prompt"], inputs["mem_sample"]
    segs = _segments(xp, xs_, mp, ms, seg_len)
    assert len(segs) == n_cores * n_seg
    ident, dist, iota = _consts()
    shared = {
        "c_ident": ident, "c_dist": dist, "c_iota": iota,
        "w_in": np.ascontiguousarray(inputs["w_in"][0]),
        "sink": np.ascontiguousarray(inputs["sink"][0]).reshape(1, 8),
        "conv_w": np.ascontiguousarray(inputs["conv_w"][0].reshape(3, 8, 128).transpose(2, 0, 1).reshape(128, 24)),
        "w_mem_k": np.ascontiguousarray(inputs["w_mem_k"][0]),
        "w_mem_v": np.ascontiguousarray(inputs["w_mem_v"][0]),
        "w_attn_o": np.ascontiguousarray(inputs["w_attn_o"][0]),
        "w_conv_out": np.ascontiguousarray(inputs["w_conv_out"][0]),
        "w_mem_o": np.ascontiguousarray(inputs["w_mem_o"][0]),
        "w_out": np.ascontiguousarray(inputs["w_out"][0]),
        "ln1_g": np.ascontiguousarray(inputs["ln1_g"][0]).reshape(1, D),
        "ln1_b": np.ascontiguousarray(inputs["ln1_b"][0]).reshape(1, D),
        "w_peer_q": np.ascontiguousarray(inputs["w_peer_q"][0]),
        "peer_keys": np.ascontiguousarray(inputs["peer_keys"][0]).reshape(16, 128, 128),
        "peer_u": np.ascontiguousarray(inputs["peer_u"][0]),
        "peer_v": np.ascontiguousarray(inputs["peer_v"][0]),
        "ln2_g": np.ascontiguousarray(inputs["ln2_g"][0]).reshape(1, D),
        "ln2_b": np.ascontiguousarray(inputs["ln2_b"][0]).reshape(1, D),
    }
    in_maps = []
    for c in range(n_cores):
        xcore = np.zeros((n_seg, seg_len + 256, D), np.float32)
        mcore = np.zeros((n_seg, 256, D), np.float32)
        hv = np.zeros((128, n_seg * 2 * 128), np.float32)
        for si in range(n_seg):
            kind, b, s0, S_ = segs[c * n_seg + si]
            x = xp if kind == "p" else xs_
            m = mp if kind == "p" else ms
            lo, hi = s0 - 128, s0 + seg_len + 128
            clo, chi = max(lo, 0), min(hi, S_)
            xcore[si, clo - lo:chi - lo] = x[b, clo:chi]
            mcore[si] = m[b]
            if lo >= 0:
                hv[:, (si * 2 + 0) * 128:(si * 2 + 1) * 128] = 1.0
            if hi <= S_:
                hv[:, (si * 2 + 1) * 128:(si * 2 + 2) * 128] = 1.0
        d = dict(shared)
        d["xc"] = xcore
        d["memc"] = mcore
        d["hval"] = hv
        in_maps.append(d)
    return in_maps, segs


def run(inputs, n_cores=NCORES, n_seg=3, seg_len=SEG, debug_h=False):
    nc = build_program(n_seg=n_seg, seg_tiles=seg_len // 128, debug_h=debug_h)
    in_maps, segs = make_in_maps(inputs, n_cores, n_seg, seg_len)
    res = run_bass_kernel_spmd(nc, in_maps, core_ids=list(range(n_cores)))
    yp = np.zeros(inputs["x_prompt"].shape, np.float32)
    ys = np.zeros(inputs["x_sample"].shape, np.float32)
    for c in range(n_cores):
        y = res.results[c]["yout"]
        for si in range(n_seg):
            kind, b, s0, S_ = segs[c * n_seg + si]
            dst = yp if kind == "p" else ys
            dst[b, s0:s0 + seg_len] = y[si * seg_len:(si + 1) * seg_len]
    return yp, ys


def kernel(**inputs):
    inputs = {k: np.asarray(v) for k, v in inputs.items()}
    return run(inputs)
```
